# Optimizing a Trainium2 kernel written in Bass

```python
import math
import jax
import jax.numpy as jnp
from jax import lax
import numpy as np

D_MODEL = 1024
BATCH = 16
SEQ = 2048
DEPTH = 1
DEC_BATCH = 2
DEC_SEQ = 16384
PAST_LEN = 128

D_MIX = D_MODEL
S5_WIDTH = D_MIX // 2
S5_CH = 16
S5_GROUPS = S5_WIDTH // S5_CH
S5_STATE = 64
ATTN_WIDTH = D_MIX - S5_WIDTH
HEAD_DIM = 64
N_HEADS = ATTN_WIDTH // HEAD_DIM
DILATED_PATTERNS = ((128, 1), (512, 4), (2048, 16))
ATTN_BLOCK = 64
D_IN = S5_WIDTH + 3 * ATTN_WIDTH
D_FF = 2816
CONV_WIDTH = 3
NORM_EPS = 1e-6
NEG_INF = -1e30

kernel_name = "hymba_s5_longnet_encoder"


def rms_norm(x, g):
    x32 = x.astype(jnp.float32)
    return x32 * lax.rsqrt(jnp.mean(x32 * x32, axis=-1, keepdims=True) + NORM_EPS) * g.astype(jnp.float32)


def _complex_scan_combine(e1, e2):
    a1r, a1i, b1r, b1i = e1
    a2r, a2i, b2r, b2i = e2
    ar = a1r * a2r - a1i * a2i
    ai = a1r * a2i + a1i * a2r
    br = a2r * b1r - a2i * b1i + b2r
    bi = a2r * b1i + a2i * b1r + b2i
    return (ar, ai, br, bi)


def s5_mixer(u, a_re, a_im, log_dt, b_re, b_im, c_re, c_im, d_skip, w_glu, b_glu):
    bsz, seq, _ = u.shape
    u32 = u.astype(jnp.float32)
    ug = u32.reshape(bsz, seq, S5_GROUPS, S5_CH)
    y = u32 * d_skip.astype(jnp.float32)
    for direction in range(2):
        ar = a_re[direction].astype(jnp.float32)
        ai = a_im[direction].astype(jnp.float32)
        dt = jnp.exp(log_dt[direction].astype(jnp.float32))[:, None]
        mag = jnp.exp(ar * dt)
        lr = mag * jnp.cos(ai * dt)
        li = mag * jnp.sin(ai * dt)
        den = ar * ar + ai * ai
        nr = lr - 1.0
        fr = (nr * ar + li * ai) / den
        fi = (li * ar - nr * ai) / den
        br = b_re[direction].astype(jnp.float32)
        bi = b_im[direction].astype(jnp.float32)
        bbar_re = fr[:, :, None] * br - fi[:, :, None] * bi
        bbar_im = fr[:, :, None] * bi + fi[:, :, None] * br
        bu_re = jnp.einsum('bsgc,gpc->bsgp', ug, bbar_re)
        bu_im = jnp.einsum('bsgc,gpc->bsgp', ug, bbar_im)
        lam_re = jnp.broadcast_to(lr, bu_re.shape)
        lam_im = jnp.broadcast_to(li, bu_im.shape)
        _, _, h_re, h_im = lax.associative_scan(
            _complex_scan_combine, (lam_re, lam_im, bu_re, bu_im), axis=1, reverse=(direction == 1))
        yd = (jnp.einsum('bsgp,gcp->bsgc', h_re, c_re[direction].astype(jnp.float32))
              - jnp.einsum('bsgp,gcp->bsgc', h_im, c_im[direction].astype(jnp.float32)))
        y = y + yd.reshape(bsz, seq, S5_WIDTH)
    z = jax.nn.gelu(y)
    gate = jax.nn.sigmoid(z @ w_glu.astype(jnp.float32) + b_glu.astype(jnp.float32))
    return z * gate


def dilated_branch(q, k, v, window, dil, slopes):
    bsz, seq, nh, hd = q.shape
    sub_len = seq // dil
    half = window // (2 * dil)
    nb = -(-sub_len // ATTN_BLOCK)
    lp = nb * ATTN_BLOCK

    def to_sub(t):
        return t.reshape(bsz, sub_len, dil, nh, hd).transpose(0, 2, 1, 3, 4)

    qb = jnp.pad(to_sub(q), ((0, 0), (0, 0), (0, lp - sub_len), (0, 0), (0, 0)))
    qb = qb.reshape(bsz, dil, nb, ATTN_BLOCK, nh, hd)

    def key_blocks(t):
        tp = jnp.pad(to_sub(t), ((0, 0), (0, 0), (ATTN_BLOCK, lp - sub_len + ATTN_BLOCK), (0, 0), (0, 0)))
        views = [tp[:, :, o * ATTN_BLOCK:o * ATTN_BLOCK + lp].reshape(bsz, dil, nb, ATTN_BLOCK, nh, hd)
                 for o in range(3)]
        return jnp.concatenate(views, axis=3)

    kb = key_blocks(k)
    vb = key_blocks(v)
    s = jnp.einsum('bgnqhe,bgnkhe->bgnhqk', qb, kb)
    qi = jnp.arange(nb)[:, None] * ATTN_BLOCK + jnp.arange(ATTN_BLOCK)[None, :]
    kj = jnp.arange(nb)[:, None] * ATTN_BLOCK + jnp.arange(3 * ATTN_BLOCK)[None, :] - ATTN_BLOCK
    rel = jnp.abs(qi[:, :, None] - kj[:, None, :])
    valid = (rel <= half) & (kj[:, None, :] >= 0) & (kj[:, None, :] < sub_len)
    bias = -slopes[None, :, None, None] * (dil * rel).astype(jnp.float32)[:, None]
    s = jnp.where(valid[:, None], s + bias, NEG_INF)
    m = jnp.max(s, axis=-1, keepdims=True)
    p = jnp.exp(s - m)
    den = jnp.sum(p, axis=-1, keepdims=True)
    o = jnp.einsum('bgnhqk,bgnkhe->bgnqhe', p / den, vb)
    lse = (m + jnp.log(den))[..., 0].transpose(0, 1, 2, 4, 3)
    o = o.reshape(bsz, dil, lp, nh, hd)[:, :, :sub_len].transpose(0, 2, 1, 3, 4).reshape(bsz, seq, nh, hd)
    lse = lse.reshape(bsz, dil, lp, nh)[:, :, :sub_len].transpose(0, 2, 1, 3).reshape(bsz, seq, nh)
    return o, lse


def dilated_attention(q, k, v):
    slopes = jnp.exp2(-8.0 * jnp.arange(1, N_HEADS + 1, dtype=jnp.float32) / N_HEADS)
    outs, lses = [], []
    for window, dil in DILATED_PATTERNS:
        o, lse = dilated_branch(q, k, v, window, dil, slopes)
        outs.append(o)
        lses.append(lse)
    weights = jax.nn.softmax(jnp.stack(lses, axis=0), axis=0)
    return jnp.sum(weights[..., None] * jnp.stack(outs, axis=0), axis=0)


def mixing_sublayer(x, norm_g, w_in, a_re, a_im, log_dt, b_re, b_im, c_re, c_im, d_skip, w_glu, b_glu,
                    q_norm_g, k_norm_g, ssm_out_g, attn_out_g, w_out):
    bsz, seq, _ = x.shape
    n = rms_norm(x, norm_g).astype(w_in.dtype)
    proj = n @ w_in
    u = proj[..., :S5_WIDTH]
    q, k, v = jnp.split(proj[..., S5_WIDTH:], 3, axis=-1)
    a_out = s5_mixer(u, a_re, a_im, log_dt, b_re, b_im, c_re, c_im, d_skip, w_glu, b_glu)
    q = rms_norm(q.reshape(bsz, seq, N_HEADS, HEAD_DIM), q_norm_g) * (HEAD_DIM ** -0.5)
    k = rms_norm(k.reshape(bsz, seq, N_HEADS, HEAD_DIM), k_norm_g)
    v = v.reshape(bsz, seq, N_HEADS, HEAD_DIM).astype(jnp.float32)
    b_out = dilated_attention(q, k, v).reshape(bsz, seq, ATTN_WIDTH)
    mix = jnp.concatenate([rms_norm(a_out, ssm_out_g), rms_norm(b_out, attn_out_g)], axis=-1)
    return x + (mix.astype(w_out.dtype) @ w_out).astype(x.dtype)


def conv_ffn_sublayer(x, norm_g, w_up, conv_w, conv_b, w_down):
    n = rms_norm(x, norm_g).astype(w_up.dtype)
    h = n @ w_up
    ch = h.shape[-1]
    h = lax.conv_general_dilated(
        h, conv_w.astype(h.dtype)[:, None, :], window_strides=(1,), padding=[(1, 1)],
        dimension_numbers=('NWC', 'WIO', 'NWC'), feature_group_count=ch) + conv_b.astype(h.dtype)
    g, up = jnp.split(h, 2, axis=-1)
    act = jax.nn.silu(g.astype(jnp.float32)) * up.astype(jnp.float32)
    return x + (act.astype(w_down.dtype) @ w_down).astype(x.dtype)


def encoder_layer(x, i, norm_mix_g, w_in, s5_a_re, s5_a_im, s5_log_dt, s5_b_re, s5_b_im, s5_c_re, s5_c_im,
                  s5_d, w_glu, b_glu, q_norm_g, k_norm_g, ssm_out_g, attn_out_g, w_out,
                  norm_ffn_g, w_up, conv_w, conv_b, w_down):
    x = mixing_sublayer(x, norm_mix_g[i], w_in[i], s5_a_re[i], s5_a_im[i], s5_log_dt[i], s5_b_re[i],
                        s5_b_im[i], s5_c_re[i], s5_c_im[i], s5_d[i], w_glu[i], b_glu[i], q_norm_g[i],
                        k_norm_g[i], ssm_out_g[i], attn_out_g[i], w_out[i])
    x = conv_ffn_sublayer(x, norm_ffn_g[i], w_up[i], conv_w[i], conv_b[i], w_down[i])
    return x


def setup_inputs(seed: int = 0) -> dict:
    key = jax.random.key(seed)
    ks = jax.random.split(key, 24)
    f32 = jnp.float32
    nrm = lambda k, shape, scale: jax.random.normal(k, shape, f32) * scale
    n_idx = jnp.arange(S5_STATE, dtype=f32)
    return {
        "x_prompt": jax.random.normal(ks[0], (BATCH, SEQ, D_MODEL), f32),
        "x_sample": jax.random.normal(ks[1], (DEC_BATCH, DEC_SEQ, D_MODEL), f32),
        "norm_mix_g": 1.0 + nrm(ks[2], (DEPTH, D_MODEL), 0.01),
        "w_in": nrm(ks[3], (DEPTH, D_MODEL, D_IN), D_MODEL ** -0.5),
        "s5_a_re": -0.5 + nrm(ks[4], (DEPTH, 2, S5_GROUPS, S5_STATE), 0.01),
        "s5_a_im": math.pi * n_idx + nrm(ks[5], (DEPTH, 2, S5_GROUPS, S5_STATE), 0.01),
        "s5_log_dt": jax.random.uniform(ks[6], (DEPTH, 2, S5_GROUPS), f32, math.log(0.001), math.log(0.1)),
        "s5_b_re": nrm(ks[7], (DEPTH, 2, S5_GROUPS, S5_STATE, S5_CH), (2 * S5_CH) ** -0.5),
        "s5_b_im": nrm(ks[8], (DEPTH, 2, S5_GROUPS, S5_STATE, S5_CH), (2 * S5_CH) ** -0.5),
        "s5_c_re": nrm(ks[9], (DEPTH, 2, S5_GROUPS, S5_CH, S5_STATE), (2 * S5_STATE) ** -0.5),
        "s5_c_im": nrm(ks[10], (DEPTH, 2, S5_GROUPS, S5_CH, S5_STATE), (2 * S5_STATE) ** -0.5),
        "s5_d": nrm(ks[11], (DEPTH, S5_WIDTH), 1.0),
        "w_glu": nrm(ks[12], (DEPTH, S5_WIDTH, S5_WIDTH), S5_WIDTH ** -0.5),
        "b_glu": nrm(ks[13], (DEPTH, S5_WIDTH), 0.01),
        "q_norm_g": 1.0 + nrm(ks[14], (DEPTH, HEAD_DIM), 0.01),
        "k_norm_g": 1.0 + nrm(ks[15], (DEPTH, HEAD_DIM), 0.01),
        "ssm_out_g": 1.0 + nrm(ks[16], (DEPTH, S5_WIDTH), 0.01),
        "attn_out_g": 1.0 + nrm(ks[17], (DEPTH, ATTN_WIDTH), 0.01),
        "w_out": nrm(ks[18], (DEPTH, D_MIX, D_MODEL), D_MIX ** -0.5),
        "norm_ffn_g": 1.0 + nrm(ks[19], (DEPTH, D_MODEL), 0.01),
        "w_up": nrm(ks[20], (DEPTH, D_MODEL, 2 * D_FF), D_MODEL ** -0.5),
        "conv_w": nrm(ks[21], (DEPTH, CONV_WIDTH, 2 * D_FF), CONV_WIDTH ** -0.5),
        "conv_b": nrm(ks[22], (DEPTH, 2 * D_FF), 0.01),
        "w_down": nrm(ks[23], (DEPTH, D_FF, D_MODEL), D_FF ** -0.5),
    }


def reference(x_prompt, x_sample, norm_mix_g, w_in, s5_a_re, s5_a_im, s5_log_dt, s5_b_re, s5_b_im,
              s5_c_re, s5_c_im, s5_d, w_glu, b_glu, q_norm_g, k_norm_g, ssm_out_g, attn_out_g, w_out,
              norm_ffn_g, w_up, conv_w, conv_b, w_down):
    y_prompt = x_prompt
    y_sample = x_sample
    for i in range(DEPTH):
        y_prompt = encoder_layer(y_prompt, i, norm_mix_g, w_in, s5_a_re, s5_a_im, s5_log_dt, s5_b_re, s5_b_im,
                                 s5_c_re, s5_c_im, s5_d, w_glu, b_glu, q_norm_g, k_norm_g, ssm_out_g,
                                 attn_out_g, w_out, norm_ffn_g, w_up, conv_w, conv_b, w_down)
        y_sample = encoder_layer(y_sample, i, norm_mix_g, w_in, s5_a_re, s5_a_im, s5_log_dt, s5_b_re, s5_b_im,
                                 s5_c_re, s5_c_im, s5_d, w_glu, b_glu, q_norm_g, k_norm_g, ssm_out_g,
                                 attn_out_g, w_out, norm_ffn_g, w_up, conv_w, conv_b, w_down)
    return (y_prompt, y_sample)
```

```python
import math
import numpy as np
from contextlib import ExitStack
import concourse.bass as bass
import concourse.mybir as mybir
from concourse.bass_utils import run_bass_kernel_spmd

F32 = mybir.dt.float32
BF16 = mybir.dt.bfloat16
I32 = mybir.dt.int32
ALU = mybir.AluOpType
AF = mybir.ActivationFunctionType
AX = mybir.AxisListType

NCORES = 8
D = 1024
DFF = 2816
NFC = DFF // 128
HALO = 1152
EPS = 1e-6
T5 = 128
PFX = 12288
GB = 256
DOMS = [
    dict(name="p0", N=2048, ext=0, comp_lo=HALO, comp_hi=HALO + 2048, carry=False),
    dict(name="p1", N=2048, ext=0, comp_lo=HALO, comp_hi=HALO + 2048, carry=False),
    dict(name="s", N=4096, ext=128, comp_lo=0, comp_hi=4096 + 2 * HALO, carry=True),
]
for _d in DOMS:
    _d["NE"] = _d["N"] + 2 * HALO
    _d["p_lo"] = HALO - _d["ext"]
    _d["NP"] = _d["N"] + 2 * _d["ext"]

DEBUG = {"scratch_out": False, "phases": None, "doms": None}
ALL_DOMS = None

EPOCH = 20000
ENGS = ["tensor", "vector", "scalar", "gpsimd", "sync"]
COMPUTE = ["tensor", "vector", "scalar", "gpsimd"]


def sl(start, n, step=1):
    return slice(start, start + (n - 1) * step + 1, step)


class Buf:
    __slots__ = ("name", "w", "r")

    def __init__(self, name=""):
        self.name = name
        self.w = None
        self.r = {}


class Prog:
    def __init__(self, nc, stack, n_eng_sems=5, n_dma_sems=16, same_engine_sync=True):
        self.nc = nc
        self.same_engine_sync = same_engine_sync
        self.ops = {e: [] for e in ENGS}
        self.cnt = {e: 0 for e in COMPUTE}
        self.esems = {e: [stack.enter_context(nc.semaphore(f"s_{e}_{i}")) for i in range(n_eng_sems)]
                      for e in COMPUTE}
        self.dsems = {q: [stack.enter_context(nc.semaphore(f"d_{q}_{i}")) for i in range(n_dma_sems)]
                      for q in ("sync", "gpsimd")}
        self.dval = {q: [0] * n_dma_sems for q in ("sync", "gpsimd")}
        self.dnext = {q: 0 for q in ("sync", "gpsimd")}
        self.known = {e: {} for e in ENGS}
        self.latest = {}
        self.out_events = []

    def _deps(self, reads, writes):
        deps = []
        for b in reads:
            if b.w is not None:
                deps.append(b.w + (True,))
        for b in writes:
            if b.w is not None:
                deps.append(b.w + (False,))
            deps.extend(ev + (False,) for ev in b.r.values())
        return deps

    def _waits(self, eng, deps):
        ws = {}
        kn = self.known[eng]
        for dep in deps:
            s, v, e = dep[0], dep[1], dep[2]
            raw = dep[3] if len(dep) > 3 else True
            if e == eng:
                if eng == "tensor" or not raw or not self.same_engine_sync:
                    continue
            key = id(s)
            if kn.get(key, 0) >= v:
                continue
            if key not in ws or ws[key][1] < v:
                ws[key] = (s, v)
        for key, (s, v) in ws.items():
            kn[key] = v
        return list(ws.values())

    def _commit(self, ev, reads, writes):
        self.latest[id(ev[0])] = ev
        for b in reads:
            old = b.r.get(id(ev[0]))
            if old is None or old[1] < ev[1]:
                b.r[id(ev[0])] = ev
        for b in writes:
            b.w = ev
            b.r = {}

    def op(self, eng, fn, reads=(), writes=()):
        deps = self._deps(reads, writes)
        waits = self._waits(eng, deps)
        self.cnt[eng] += 1
        c = self.cnt[eng]
        s = self.esems[eng][(c - 1) // EPOCH]
        v = (c - 1) % EPOCH + 1
        ev = (s, v, eng)
        self.ops[eng].append((fn, waits, (s, 1)))
        self._commit(ev, reads, writes)
        return ev

    def op_group(self, eng, fns, reads=(), writes=()):
        deps = self._deps(reads, writes)
        waits = self._waits(eng, deps)
        self.cnt[eng] += 1
        c = self.cnt[eng]
        s = self.esems[eng][(c - 1) // EPOCH]
        v = (c - 1) % EPOCH + 1
        ev = (s, v, eng)
        n = len(fns)
        for i, fn in enumerate(fns):
            self.ops[eng].append((fn, waits if i == 0 else [], (s, 1) if i == n - 1 else None))
        self._commit(ev, reads, writes)
        return ev

    def dma(self, q, out, in_, reads=(), writes=(), is_output=False, **kw):
        deps = self._deps(reads, writes)
        i = self.dnext[q]
        self.dnext[q] = (i + 1) % len(self.dsems[q])
        s = self.dsems[q][i]
        if self.dval[q][i] > 0:
            deps.append((s, self.dval[q][i], "dma", True))
        waits = self._waits(q, deps)
        self.dval[q][i] += 16
        ev = (s, self.dval[q][i], "dma")
        self.ops[q].append((lambda e: e.dma_start(out=out, in_=in_, **kw), waits, (s, 16)))
        self._commit(ev, reads, writes)
        if is_output:
            self.out_events.append(ev)
        return ev

    def barrier(self):
        evs = list(self.latest.values())
        for eng in ENGS:
            waits = self._waits(eng, [(s, v, "x") for (s, v, e) in evs])
            if waits:
                self.ops[eng].append((None, waits, None))

    def finish(self):
        self.barrier()

    def emit(self):
        nc = self.nc
        with nc.Block() as block:
            def run(e, lst):
                for fn, waits, inc in lst:
                    for (s, v) in waits:
                        e.wait_ge(s, v)
                    if fn is not None:
                        ins = fn(e)
                        if inc is not None:
                            ins.then_inc(inc[0], inc[1])

            @block.tensor
            def _(e):
                run(e, self.ops["tensor"])

            @block.vector
            def _(e):
                run(e, self.ops["vector"])

            @block.scalar
            def _(e):
                run(e, self.ops["scalar"])

            @block.gpsimd
            def _(e):
                run(e, self.ops["gpsimd"])

            @block.sync
            def _(e):
                run(e, self.ops["sync"])


class K:
    def __init__(self, P):
        self.P = P

    def mm(self, out, lhsT, rhs, start, stop, r, w):
        return self.P.op("tensor", lambda e: e.matmul(out, lhsT=lhsT, rhs=rhs, start=start, stop=stop,
                                                      skip_group_check=True), r, w)

    def mm_group(self, items, r, w):
        fns = [(lambda e, o=o, l=l, rh=rh, st=st, sp=sp: e.matmul(o, lhsT=l, rhs=rh, start=st, stop=sp,
                                                                   skip_group_check=True))
               for (o, l, rh, st, sp) in items]
        return self.P.op_group("tensor", fns, r, w)

    def tr(self, out, in_, ident, r, w):
        return self.P.op("tensor", lambda e: e.transpose(out, in_, ident), r, w)

    def act(self, out, in_, func, r, w, scale=1.0, bias=0.0, accum=None):
        if accum is None:
            return self.P.op("scalar", lambda e: e.activation(out=out, in_=in_, func=func, scale=scale, bias=bias), r, w)
        return self.P.op("scalar", lambda e: e.activation(out=out, in_=in_, func=func, scale=scale, bias=bias,
                                                          accum_out=accum), r, w)

    def tt(self, eng, out, in0, in1, op, r, w):
        return self.P.op(eng, lambda e: e.tensor_tensor(out=out, in0=in0, in1=in1, op=op), r, w)

    def ts(self, eng, out, in0, s1, s2, op0, op1, r, w):
        if op1 is None:
            return self.P.op(eng, lambda e: e.tensor_scalar(out=out, in0=in0, scalar1=s1, scalar2=None, op0=op0), r, w)
        return self.P.op(eng, lambda e: e.tensor_scalar(out=out, in0=in0, scalar1=s1, scalar2=s2, op0=op0, op1=op1), r, w)

    def stt(self, out, in0, scalar, in1, op0, op1, r, w):
        return self.P.op("vector", lambda e: e.scalar_tensor_tensor(out=out, in0=in0, scalar=scalar, in1=in1,
                                                                    op0=op0, op1=op1), r, w)

    def cp(self, eng, out, in_, r, w):
        if eng == "scalar":
            return self.P.op("scalar", lambda e: e.copy(out=out, in_=in_), r, w)
        return self.P.op(eng, lambda e: e.tensor_copy(out=out, in_=in_), r, w)

    def memset(self, eng, ap, val, w):
        return self.P.op(eng, lambda e: e.memset(ap, val), (), w)

    def recip(self, out, in_, r, w):
        return self.P.op("vector", lambda e: e.reciprocal(out=out, in_=in_), r, w)

    def red(self, out, in_, r, w):
        return self.P.op("vector", lambda e: e.tensor_reduce(out=out, in_=in_, axis=AX.X, op=ALU.add), r, w)

    def scan(self, out, d0, d1, r, w):
        return self.P.op("vector", lambda e: e.tensor_tensor_scan(out=out, data0=d0, data1=d1, initial=0.0,
                                                                  op0=ALU.mult, op1=ALU.add), r, w)


class Ring:
    def __init__(self, tiles):
        self.tiles = tiles
        self.bufs = [Buf() for _ in tiles]
        self.i = 0

    def next(self):
        t, b = self.tiles[self.i], self.bufs[self.i]
        self.i = (self.i + 1) % len(self.tiles)
        return t, b


def host_layout(inp):
    f32 = np.float32
    xp = np.asarray(inp["x_prompt"], f32)
    xs = np.asarray(inp["x_sample"], f32)
    shared = {}
    shared["w_in"] = np.ascontiguousarray(inp["w_in"][0], f32)
    shared["w_out"] = np.ascontiguousarray(inp["w_out"][0], f32)
    shared["w_up"] = np.ascontiguousarray(inp["w_up"][0], f32)
    shared["w_down"] = np.ascontiguousarray(inp["w_down"][0], f32)
    shared["w_glu"] = np.ascontiguousarray(inp["w_glu"][0], f32)

    def pk(v, k):
        return np.ascontiguousarray(np.asarray(v, f32).reshape(k, 128).T)

    shared["g_mix"] = pk(inp["norm_mix_g"][0], 8)
    shared["g_ffn"] = pk(inp["norm_ffn_g"][0], 8)
    shared["g_out"] = pk(np.concatenate([inp["ssm_out_g"][0], inp["attn_out_g"][0]]), 8)
    shared["b_glu"] = pk(inp["b_glu"][0], 4)
    shared["s5_d"] = pk(inp["s5_d"][0], 4)
    shared["qg"] = np.ascontiguousarray(np.broadcast_to(np.asarray(inp["q_norm_g"][0], f32)[None, :], (128, 64)))
    shared["kg"] = np.ascontiguousarray(np.broadcast_to(np.asarray(inp["k_norm_g"][0], f32)[None, :], (128, 64)))
    cw = np.asarray(inp["conv_w"][0], f32)
    shared["conv_w"] = np.ascontiguousarray(cw.reshape(3, 44, 128).transpose(2, 1, 0))
    shared["conv_b"] = pk(inp["conv_b"][0], 44)
    def sm(a):
        a = np.asarray(a, f32).reshape(2, 16, 2, 64)
        return np.ascontiguousarray(a.transpose(0, 2, 3, 1).reshape(2, 128, 16))
    shared["a_re"] = sm(inp["s5_a_re"][0])
    shared["a_im"] = sm(inp["s5_a_im"][0])
    ldt = np.broadcast_to(np.asarray(inp["s5_log_dt"][0], f32)[:, :, None], (2, 32, 64))
    shared["log_dt"] = sm(ldt)
    for nm in ("re", "im"):
        b = np.asarray(inp["s5_b_" + nm][0], f32)
        c = np.asarray(inp["s5_c_" + nm][0], f32)
        BL = np.zeros((2, 128, 16, 128), f32)
        CL = np.zeros((2, 128, 16, 128), f32)
        BG = np.zeros((2, 128, 16, 32), f32)
        for j in range(16):
            i = j % 4
            for two in range(2):
                g = 2 * j + two
                r0 = (2 * i + two) * 16
                BL[:, r0:r0 + 16, j, two * 64:(two + 1) * 64] = b[:, g].transpose(0, 2, 1)
                CL[:, two * 64:(two + 1) * 64, j, r0:r0 + 16] = c[:, g].transpose(0, 2, 1)
                BG[:, two * 64:(two + 1) * 64, j, two * 16:(two + 1) * 16] = b[:, g]
        shared["bl_" + nm] = BL
        shared["cl_" + nm] = CL
        shared["bg_" + nm] = BG
    kk = np.arange(128)[:, None]
    qq = np.arange(128)[None, :]
    rel = np.stack([np.abs(128 * m + kk - 64 - qq) for m in range(2)]).astype(f32)
    shared["absrel"] = np.ascontiguousarray(rel.transpose(1, 0, 2))
    shared["val01"] = np.ascontiguousarray((rel <= 64).astype(f32).transpose(1, 0, 2))
    shared["ident"] = np.eye(128, dtype=f32)
    vp = np.zeros((2048 + 2 * HALO, 1), f32)
    vp[HALO:HALO + 2048] = 1.0
    shared["valid_p"] = vp
    maps = []
    for c in range(NCORES):
        m = dict(shared)
        m["x_p0"] = np.ascontiguousarray(xp[2 * c])
        m["x_p1"] = np.ascontiguousarray(xp[2 * c + 1])
        si, q = c // 4, c % 4
        t0 = 4096 * q
        xe = np.zeros((4096 + 2 * HALO, D), f32)
        ve = np.zeros((4096 + 2 * HALO, 1), f32)
        lo, hi = max(t0 - HALO, 0), min(t0 + 4096 + HALO, 16384)
        xe[lo - (t0 - HALO):hi - (t0 - HALO)] = xs[si, lo:hi]
        ve[lo - (t0 - HALO):hi - (t0 - HALO)] = 1.0
        m["x_s"] = xe
        m["valid_s"] = ve
        pf = np.zeros((PFX, D), f32)
        L = max(t0 - 128, 0)
        if L:
            pf[PFX - L:] = xs[si, 0:L]
        sf = np.zeros((PFX, D), f32)
        st_ = t0 + 4096 + 128
        L2 = max(16384 - st_, 0)
        if L2:
            sf[:L2] = xs[si, st_:]
        m["x_pf"] = pf
        m["x_sf"] = sf
        maps.append(m)
    return maps


INPUT_SHAPES = {
    "w_in": [D, 2048], "w_out": [D, D], "w_up": [D, 2 * DFF], "w_down": [DFF, D], "w_glu": [512, 512],
    "g_mix": [128, 8], "g_ffn": [128, 8], "g_out": [128, 8], "b_glu": [128, 4], "s5_d": [128, 4],
    "qg": [128, 64], "kg": [128, 64], "conv_w": [128, 44, 3], "conv_b": [128, 44],
    "a_re": [2, 128, 16], "a_im": [2, 128, 16], "log_dt": [2, 128, 16],
    "bl_re": [2, 128, 16, 128], "bl_im": [2, 128, 16, 128], "cl_re": [2, 128, 16, 128], "cl_im": [2, 128, 16, 128],
    "bg_re": [2, 128, 16, 32], "bg_im": [2, 128, 16, 32],
    "absrel": [128, 2, 128], "val01": [128, 2, 128], "ident": [128, 128],
    "valid_p": [2048 + 2 * HALO, 1], "valid_s": [4096 + 2 * HALO, 1],
    "x_p0": [2048, D], "x_p1": [2048, D], "x_s": [4096 + 2 * HALO, D], "x_pf": [PFX, D], "x_sf": [PFX, D],
}


class Ctx:
    pass


def _h8(ap):
    return ap.rearrange("p (h e) -> p h e", h=8)


def phase_proj(C):
    nc, P, k = C.nc, C.P, C.k
    with ExitStack() as st:
        T = lambda n, s, d=F32: st.enter_context(nc.sbuf_tensor("pj_" + n, s, d))
        PS = lambda n, s, d=F32: st.enter_context(nc.psum_tensor("pj_" + n, s, d))
        win = T("win", [128, 8, 2048], BF16); bwin = Buf()
        gm = T("gm", [128, 8]); bgm = Buf()
        P.dma("sync", gm[:], C.inp["g_mix"][:, :], writes=[bgm])
        wst = Ring([T(f"wst{i}", [128, 2048]) for i in range(2)])
        for kk in range(8):
            t, b = wst.next()
            P.dma("sync", t[:], C.inp["w_in"][kk * 128:(kk + 1) * 128, :], writes=[b])
            k.ts("vector", win[:, kk, :], t[:], gm[:, kk:kk + 1], None, ALU.mult, None, [b, bgm], [bwin])
        qg = T("qg", [128, 64]); bqg = Buf()
        kg = T("kg", [128, 64]); bkg = Buf()
        P.dma("sync", qg[:], C.inp["qg"][:, :], writes=[bqg])
        P.dma("sync", kg[:], C.inp["kg"][:, :], writes=[bkg])
        k.ts("vector", qg[:], qg[:], 0.125, None, ALU.mult, None, [bqg], [bqg])
        zt = T("zt", [128, 4680], BF16); bzt = Buf()
        k.memset("vector", zt[:], 0.0, [bzt])

        xr = Ring([T(f"x{i}", [128, 1024]) for i in range(3)])
        vr = Ring([T(f"vl{i}", [128, 1]) for i in range(3)])
        junk = T("junk", [128, 1024], BF16); bjunk = Buf()
        ssr = Ring([T(f"ss{i}", [128, 4]) for i in range(3)])
        nbr = Ring([T(f"nb{i}", [128, 1024], BF16) for i in range(2)])
        nTs = [T(f"nT{i}", [128, 8, 512], BF16) for i in range(2)]
        nTb = [[Buf() for _ in range(4)] for _ in range(2)]
        sqr = Ring([T(f"sq{i}", [128, 512]) for i in range(2)])
        hsr = Ring([T(f"hs{i}", [128, 24]) for i in range(4)])
        tmr = Ring([T(f"tm{i}", [128, 512]) for i in range(2)])
        qnr = Ring([T(f"qn{i}", [128, 512], BF16) for i in range(2)])
        var = Ring([T(f"va{i}", [128, 8, 65], BF16) for i in range(3)])
        qts = [T(f"qts{i}", [128, 4, 512], BF16) for i in range(2)]; bqts = [Buf(), Buf()]
        kts = [T(f"kts{i}", [128, 4, 512], BF16) for i in range(2)]; bkts = [Buf(), Buf()]
        uts = [T(f"uts{i}", [128, 4, 512], BF16) for i in range(2)]; buts = [Buf(), Buf()]
        ptr = PS("ptr", [128, 8, 128], BF16); bptr = Buf()
        pq = PS("pq", [128, 512]); bpq = Buf()
        pk = PS("pk", [128, 512]); bpk = Buf()
        pv = PS("pv", [128, 512]); bpv = Buf()
        pqt = PS("pqt", [128, 2, 4, 128], BF16); bpqt = [Buf(), Buf()]
        pur = Ring([PS(f"pu{i}", [128, 512]) for i in range(2)])
        identb, bid = C.identb, C.bid
        blk_i = 0
        for dom in DOMS:
            S = C.scr[dom["name"]]
            xsrc = C.inp["x_" + dom["name"]]
            vsrc = C.inp["valid_s" if dom["name"] == "s" else "valid_p"]
            NE = dom["NE"]
            lo, hi = dom["comp_lo"], dom["comp_hi"]
            KTv = S["KT"].rearrange("(c p) e -> p c e", p=128)
            QTv = S["QT"].rearrange("(c p) e -> p c e", p=128)
            UTv = S["UT"].rearrange("(c p) e -> p c e", p=128)
            if lo > 0:
                for (a, b_) in ((0, lo), (hi, NE)):
                    n = b_ - a
                    P.dma("gpsimd", KTv[:, :, a:b_], zt[:, 0:4 * n].rearrange("p (c e) -> p c e", c=4), reads=[bzt])
                    P.dma("gpsimd", S["VA"][a:b_, :].rearrange("(p a) c -> p a c", p=128),
                          zt[:, 0:(n // 128) * 520].rearrange("p (a c) -> p a c", c=520), reads=[bzt])
            e0 = lo
            while e0 < hi:
                nt = min(4, (hi - e0) // 128)
                W = nt * 128
                bi = blk_i % 2
                blk_i += 1
                nTt = nTs[bi]
                for i in range(nt):
                    er = e0 + i * 128
                    xt, bx = xr.next()
                    P.dma("sync", xt[:], xsrc[er - lo:er - lo + 128, :], writes=[bx])
                    vt, bvt = vr.next()
                    P.dma("sync", vt[:], vsrc[er:er + 128, :], writes=[bvt])
                    ss, bss = ssr.next()
                    k.act(junk[:], xt[:], AF.Square, [bx], [bjunk, bss], accum=ss[:, 0:1])
                    k.act(ss[:, 1:2], ss[:, 0:1], AF.Sqrt, [bss], [bss], scale=1.0 / D, bias=EPS)
                    k.recip(ss[:, 2:3], ss[:, 1:2], [bss], [bss])
                    nbt, bnb = nbr.next()
                    k.act(nbt[:], xt[:], AF.Copy, [bx, bss], [bnb], scale=ss[:, 2:3])
                    for kk in range(8):
                        k.tr(ptr[:, kk, :], nbt[:, kk * 128:(kk + 1) * 128], identb[:], [bnb, bid], [bptr])
                    bnTi = nTb[bi][i]
                    k.cp("vector", nTt[:, :, i * 128:(i + 1) * 128], ptr[:], [bptr], [bnTi])
                    for c, (pt, bpt) in enumerate(((pq, bpq), (pk, bpk), (pv, bpv))):
                        k.mm_group([(pt[:], nTt[:, kk, i * 128:(i + 1) * 128], win[:, kk, 512 * (c + 1):512 * (c + 2)],
                                     kk == 0, kk == 7) for kk in range(8)], [bnTi, bwin], [bpt])
                    for which, (pt, bpt, gt, bg, stg, bstg) in enumerate((
                            (pq, bpq, qg, bqg, qts[bi], bqts[bi]), (pk, bpk, kg, bkg, kts[bi], bkts[bi]))):
                        sqf, bsq = sqr.next()
                        k.act(sqf[:], pt[:], AF.Square, [bpt], [bsq])
                        hs, bhs = hsr.next()
                        k.red(hs[:, 0:8], _h8(sqf[:]), [bsq], [bhs])
                        k.act(hs[:, 8:16], hs[:, 0:8], AF.Sqrt, [bhs], [bhs], scale=1.0 / 64, bias=EPS)
                        k.recip(hs[:, 16:24], hs[:, 8:16], [bhs], [bhs])
                        tmp, btm = tmr.next()
                        k.tt("vector", _h8(tmp[:]), _h8(pt[:]), hs[:, 16:24].unsqueeze(2).to_broadcast([128, 8, 64]),
                             ALU.mult, [bpt, bhs], [btm])
                        qn, bqn = qnr.next()
                        k.tt("vector", _h8(qn[:]), _h8(tmp[:]), gt[:].unsqueeze(1).to_broadcast([128, 8, 64]),
                             ALU.mult, [btm, bg], [bqn])
                        for c4 in range(4):
                            k.tr(pqt[:, which, c4, :], qn[:, c4 * 128:(c4 + 1) * 128], identb[:], [bqn, bid],
                                 [bpqt[which]])
                        k.cp("scalar", stg[:, :, i * 128:(i + 1) * 128], pqt[:, which, :, :], [bpqt[which]], [bstg])
                    va, bva = var.next()
                    k.cp("scalar", va[:, :, 0:64], _h8(pv[:]), [bpv], [bva])
                    k.cp("vector", va[:, :, 64:65], vt[:, 0:1].unsqueeze(1).to_broadcast([128, 8, 1]), [bvt], [bva])
                    P.dma("gpsimd", S["VA"][er:er + 128, :], va[:].rearrange("p h e -> p (h e)"), reads=[bva])
                for c in range(4):
                    pu, bpu = pur.next()
                    k.mm_group([(pu[:, 0:W], win[:, kk, c * 128:(c + 1) * 128], nTt[:, kk, 0:W], kk == 0, kk == 7)
                                for kk in range(8)], [bwin] + nTb[bi][0:nt], [bpu])
                    k.cp("scalar" if c % 2 else "vector", uts[bi][:, c, 0:W], pu[:, 0:W], [bpu], [buts[bi]])
                P.dma("gpsimd", UTv[:, :, e0:e0 + W], uts[bi][:, :, 0:W], reads=[buts[bi]])
                P.dma("gpsimd", QTv[:, :, e0:e0 + W], qts[bi][:, :, 0:W], reads=[bqts[bi]])
                P.dma("gpsimd", KTv[:, :, e0:e0 + W], kts[bi][:, :, 0:W], reads=[bkts[bi]])
                e0 += W
        if any(d_["carry"] for d_ in DOMS):
            for srcn, dstn in (("x_pf", "UTPF"), ("x_sf", "UTSF")):
                xsrc = C.inp[srcn]
                for e0 in range(0, PFX, 512):
                    bi = blk_i % 2
                    blk_i += 1
                    nTt = nTs[bi]
                    for i in range(4):
                        er = e0 + i * 128
                        xt, bx = xr.next()
                        P.dma("sync", xt[:], xsrc[er:er + 128, :], writes=[bx])
                        ss, bss = ssr.next()
                        k.act(junk[:], xt[:], AF.Square, [bx], [bjunk, bss], accum=ss[:, 0:1])
                        k.act(ss[:, 1:2], ss[:, 0:1], AF.Sqrt, [bss], [bss], scale=1.0 / D, bias=EPS)
                        k.recip(ss[:, 2:3], ss[:, 1:2], [bss], [bss])
                        nbt, bnb = nbr.next()
                        k.act(nbt[:], xt[:], AF.Copy, [bx, bss], [bnb], scale=ss[:, 2:3])
                        for kk in range(8):
                            k.tr(ptr[:, kk, :], nbt[:, kk * 128:(kk + 1) * 128], identb[:], [bnb, bid], [bptr])
                        k.cp("vector", nTt[:, :, i * 128:(i + 1) * 128], ptr[:], [bptr], [nTb[bi][i]])
                    for i in range(4):
                        pu, bpu = pur.next()
                        k.mm_group([(pu[:], nTt[:, kk, i * 128:(i + 1) * 128], win[:, kk, 0:512], kk == 0, kk == 7)
                                    for kk in range(8)], [bwin, nTb[bi][i]], [bpu])
                        k.cp("scalar" if i % 2 else "vector", uts[bi][:, i, :], pu[:], [bpu], [buts[bi]])
                    P.dma("gpsimd", C.scr_pf[dstn][e0:e0 + 512, :].rearrange("(i p) c -> p i c", p=128), uts[bi][:],
                          reads=[buts[bi]])
    P.barrier()


def phase_attn(C):
    nc, P, k = C.nc, C.P, C.k
    with ExitStack() as st:
        T = lambda n, s, d=F32: st.enter_context(nc.sbuf_tensor("at_" + n, s, d))
        PS = lambda n, s, d=F32: st.enter_context(nc.psum_tensor("at_" + n, s, d))
        ar = T("absrel", [128, 2, 128]); bar_ = Buf()
        v01 = T("val01", [128, 2, 128]); bv01 = Buf()
        P.dma("sync", ar[:], C.inp["absrel"][:, :, :], writes=[bar_])
        P.dma("sync", v01[:], C.inp["val01"][:, :, :], writes=[bv01])
        mk = T("mk", [128, 3, 2, 8, 128], BF16); bmk = Buf()
        mtmp = Ring([T(f"mtmp{i}", [128, 2, 128]) for i in range(2)])
        for pi, d in enumerate((1, 4, 16)):
            for h in range(8):
                slope = 2.0 ** (-(h + 1))
                t, b = mtmp.next()
                k.act(t[:], ar[:], AF.Exp, [bar_], [b], scale=-slope * d)
                k.tt("vector", mk[:, pi, :, h, :], t[:], v01[:], ALU.mult, [b, bv01], [bmk])
        MRG = 1024
        WIN = 2048 + 2 * MRG
        KT = T("KTsb", [64, 8, WIN], BF16); bKT = Buf()
        QT = T("QTsb", [64, 8, WIN], BF16); bQT = Buf()
        vtr = Ring([T(f"vt{i}", [128, 8, 65], BF16) for i in range(4)])
        prr = Ring([T(f"pt{i}", [128, 4, 128], BF16) for i in range(8)])
        osr = Ring([T(f"os{i}", [128, 8, 65]) for i in range(3)])
        psr = Ring([PS(f"pss{i}", [128, 4, 128]) for i in range(6)])
        por = Ring([PS(f"pso{i}", [128, 512]) for i in range(2)])
        for dom in DOMS:
            S = C.scr[dom["name"]]
            for sb in range(0, dom["NP"], 2048):
                NPs = min(2048, dom["NP"] - sb)
                w0 = dom["p_lo"] + sb - MRG
                Wn = NPs + 2 * MRG
                p_lo = MRG
                P.dma("sync", KT[:, :, 0:Wn], S["KT"].rearrange("(h p) e -> p h e", p=64)[:, :, w0:w0 + Wn], writes=[bKT])
                P.dma("sync", QT[:, :, 0:Wn], S["QT"].rearrange("(h p) e -> p h e", p=64)[:, :, w0:w0 + Wn], writes=[bQT])
                for pi, d in enumerate((1, 4, 16)):
                    Od = S["O%d" % d]
                    for r in range(d):
                        a0 = p_lo // d
                        npos = NPs // d
                        blocks = [(a, min(128, a0 + npos - a)) for a in range(a0, a0 + npos, 128)]
                        vtiles = {}

                        def load_v(mi, nk):
                            vt, bvt = vtr.next()
                            e_start = w0 + r + d * (a0 - 64 + 128 * mi)
                            P.dma("sync", vt[0:nk].rearrange("p h e -> p (h e)"), S["VA"][sl(e_start, nk, d), :], writes=[bvt])
                            vtiles[mi] = (vt, bvt)

                        load_v(0, 128)
                        for bi_, (a, L) in enumerate(blocks):
                            load_v(bi_ + 1, L)
                            osb, bos = osr.next()
                            units = []
                            for hg in range(2):
                                for m in range(2):
                                    nk = 128 if m == 0 else L
                                    pss, bps = psr.next()
                                    for hh in range(4):
                                        h = hg * 4 + hh
                                        kc = sl(r + d * (a - 64 + 128 * m), nk, d)
                                        qc = sl(r + d * a, L, d)
                                        k.mm(pss[0:nk, hh, 0:L], KT[:, h, kc], QT[:, h, qc], True, True,
                                             [bKT, bQT], [bps])
                                    units.append((hg, m, nk, pss, bps))
                            pts = {}
                            for (hg, m, nk, pss, bps) in units:
                                pt, bp = prr.next()
                                k.act(pt[0:nk, :, 0:L], pss[0:nk, :, 0:L], AF.Exp, [bps], [bp])
                                k.tt("vector", pt[0:nk, :, 0:L], pt[0:nk, :, 0:L],
                                     mk[0:nk, pi, m, hg * 4:(hg + 1) * 4, 0:L], ALU.mult, [bp, bmk], [bp])
                                pts[(hg, m)] = (pt, bp, nk)
                            for hg in range(2):
                                pso_, bpso = por.next()
                                pso = pso_[:, 0:260].rearrange("p (h e) -> p h e", h=4)
                                for m in range(2):
                                    pt, bp, nk = pts[(hg, m)]
                                    vt, bvt = vtiles[bi_ + m]
                                    for hh in range(4):
                                        h = hg * 4 + hh
                                        k.mm(pso[0:L, hh, :], pt[0:nk, hh, 0:L], vt[0:nk, h, :], m == 0 and hh == 0, m == 1,
                                             [bp, bvt], [bpso])
                                k.cp("vector", osb[0:L, hg * 4:(hg + 1) * 4, :], pso[0:L, :, :], [bpso], [bos])
                            P.dma("gpsimd", Od[sl(sb + r + d * a - p_lo, L, d), :], osb[0:L].rearrange("p h e -> p (h e)"),
                                  reads=[bos])
                            del vtiles[bi_]
    P.barrier()


def scratch_spec(dom):
    NE, NP, N = dom["NE"], dom["NP"], dom["N"]
    return {
        "UT": ([512, NE], BF16), "QT": ([512, NE], BF16), "KT": ([512, NE], BF16), "VA": ([NE, 520], BF16),
        "O1": ([NP, 520], F32), "O4": ([NP, 520], F32), "O16": ([NP, 520], F32),
        "YF": ([512, NP], F32), "AN": ([512, NP], BF16), "X1": ([N + 256, D], F32),
    }


PHASES = ["proj", "attn", "s5prep", "prefix", "s5", "comb", "ffn"]


def build_program():
    global DOMS, ALL_DOMS
    if ALL_DOMS is None:
        ALL_DOMS = list(DOMS)
    DOMS[:] = [d_ for d_ in ALL_DOMS if DEBUG["doms"] is None or d_["name"] in DEBUG["doms"]]
    nc = bass.Bass("TRN2", target_bir_lowering=False)
    C = Ctx()
    C.nc = nc
    skip = set()
    for d_ in ALL_DOMS:
        if d_ not in DOMS:
            skip.add("x_" + d_["name"])
            if d_["name"] == "s":
                skip.update(("x_pf", "x_sf"))
    C.inp = {n: nc.dram_tensor(n, shp, F32, kind="ExternalInput").ap() for n, shp in INPUT_SHAPES.items()
             if n not in skip}
    global LAST_INP_NAMES
    LAST_INP_NAMES = list(C.inp.keys())
    C.out = {d_["name"]: nc.dram_tensor("y_" + d_["name"], [d_["N"], D], F32, kind="ExternalOutput").ap() for d_ in DOMS}
    so = DEBUG["scratch_out"] or ()
    C.scr = {}
    for d_ in DOMS:
        C.scr[d_["name"]] = {n: nc.dram_tensor(f"scr_{d_['name']}_{n}", shp, dt,
                                               kind="ExternalOutput" if f"{d_['name']}_{n}" in so else "Internal").ap()
                             for n, (shp, dt) in scratch_spec(d_).items()}
    C.scr_tab = [nc.dram_tensor(f"scr_tab{d}", [128, 5 * 16 * 128], F32, kind="Internal").ap() for d in range(2)]
    C.scr_pf = {nm: nc.dram_tensor("scr_" + nm, [PFX, 512], BF16, kind="Internal").ap() for nm in ("UTPF", "UTSF")}
    C.scr_lam = [nc.dram_tensor(f"scr_lam{d}", [128, 2 * 2 * 16 * 128], BF16, kind="Internal").ap() for d in range(2)]
    phases = DEBUG["phases"] or PHASES
    with ExitStack() as st:
        P = Prog(nc, st)
        C.P = P
        C.k = K(P)
        C.st = st
        idf = st.enter_context(nc.sbuf_tensor("idf", [128, 128], F32))
        C.identb = st.enter_context(nc.sbuf_tensor("identb", [128, 128], BF16))
        C.bid = Buf()
        P.dma("sync", idf[:], C.inp["ident"][:, :], writes=[C.bid])
        C.k.cp("vector", C.identb[:], idf[:], [C.bid], [C.bid])
        C.idf = idf
        fns = {"proj": phase_proj, "attn": phase_attn, "s5prep": phase_s5prep, "prefix": phase_prefix,
               "s5": phase_s5, "comb": phase_comb, "ffn": phase_ffn}
        for ph in PHASES:
            if ph in phases:
                fns[ph](C)
        P.finish()
        P.emit()
    return nc


_NC_CACHE = {}
LAST_INP_NAMES = []


def kernel(**inputs):
    maps = host_layout(inputs)
    key = (tuple(DEBUG["scratch_out"] or ()), tuple(DEBUG["phases"] or PHASES), tuple(DEBUG["doms"] or ()))
    if key not in _NC_CACHE:
        nc_ = build_program()
        _NC_CACHE[key] = (nc_, list(LAST_INP_NAMES))
    nc, names = _NC_CACHE[key]
    maps = [{n: m[n] for n in names} for m in maps]
    res = run_bass_kernel_spmd(nc, maps, core_ids=list(range(NCORES)))
    R = res.results
    kernel.last_results = R
    yp = np.zeros((16, 2048, D), np.float32)
    ys = np.zeros((2, 16384, D), np.float32)
    for c in range(NCORES):
        if "y_p0" in R[c]:
            yp[2 * c] = R[c]["y_p0"]
        if "y_p1" in R[c]:
            yp[2 * c + 1] = R[c]["y_p1"]
        if "y_s" in R[c]:
            ys[c // 4, 4096 * (c % 4):4096 * (c % 4 + 1)] = R[c]["y_s"]
    return (yp, ys)


def phase_s5prep(C):
    pass


def phase_prefix(C):
    pass


TWO_PI = 2.0 * math.pi
PI_SAFE = 3.141592


def phase_s5(C):
    nc, P, k = C.nc, C.P, C.k
    with ExitStack() as st:
        T = lambda n, s, d=F32: st.enter_context(nc.sbuf_tensor("s5_" + n, s, d))
        PS = lambda n, s, d=F32: st.enter_context(nc.psum_tensor("s5_" + n, s, d))
        V = "vector"
        LR = T("LR", [128, 32]); LI = T("LI", [128, 32])
        MUr = T("MUr", [128, 32]); MUi = T("MUi", [128, 32])
        BB = T("BB", [128, 2, 2, 16, 32]); bBB = Buf()
        btab = Buf()
        pst = ExitStack()
        TP = lambda n, s, d=F32: pst.enter_context(nc.sbuf_tensor("s5p_" + n, s, d))
        Ec = TP("Ec", [128, 16, 128]); Es = TP("Es", [128, 16, 128])
        t1 = TP("t1", [128, 16, 128]); t2 = TP("t2", [128, 16, 128])
        rr = TP("rr", [128, 16, 128]); ri = TP("ri", [128, 16, 128])
        tabp = TP("tabp", [128, 5, 16, 128]); btabp = Buf()
        Pre = TP("Pre", [128, 16, GB]); Pim = TP("Pim", [128, 16, GB])
        q1 = TP("q1", [128, 16, GB]); q2 = TP("q2", [128, 16, GB])
        Pbr = TP("Pbr", [128, 16, GB], BF16); Pbi = TP("Pbi", [128, 16, GB], BF16)
        lamt = TP("lamt", [128, 2, 2, 16, 128], BF16); blam = Buf()
        bgr = TP("bgr", [128, 16, 32]); bgi = TP("bgi", [128, 16, 32])
        plam = pst.enter_context(nc.psum_tensor("s5p_plam", [128, 8, 128], BF16)); bplam = Buf()
        for d in range(2):
            sm = {}
            bsm = Buf()
            for nm in ("a_re", "a_im", "log_dt"):
                sm[nm] = TP(f"{nm}{d}", [128, 16])
                P.dma("sync", sm[nm][:], C.inp[nm][d, :, :], writes=[bsm])
            def S_(nm):
                sm[nm] = TP(f"{nm}{d}", [128, 16])
                return sm[nm]
            r_, w_ = [bsm], [bsm]
            k.act(S_("dt")[:], sm["log_dt"][:], AF.Exp, r_, w_)
            k.tt(V, S_("xr")[:], sm["a_re"][:], sm["dt"][:], ALU.mult, r_, w_)
            k.act(S_("mag")[:], sm["xr"][:], AF.Exp, r_, w_)
            k.tt(V, S_("ang")[:], sm["a_im"][:], sm["dt"][:], ALU.mult, r_, w_)
            ki = pst.enter_context(nc.sbuf_tensor(f"s5p_ki{d}", [128, 16], I32))
            for nm, off in (("sin", 0.0), ("cos", math.pi / 2)):
                k.ts(V, S_("t_" + nm)[:], sm["ang"][:], 1.0 / TWO_PI, off / TWO_PI, ALU.mult, ALU.add, r_, w_)
                k.cp(V, ki[:], sm["t_" + nm][:], r_, w_)
                k.cp(V, S_("kf_" + nm)[:], ki[:], r_, w_)
                k.ts(V, S_("a2_" + nm)[:], sm["ang"][:], off, None, ALU.add, None, r_, w_)
                k.stt(S_("r_" + nm)[:], sm["kf_" + nm][:], -TWO_PI, sm["a2_" + nm][:], ALU.mult, ALU.add, r_, w_)
                k.ts(V, sm["r_" + nm][:], sm["r_" + nm][:], -PI_SAFE, PI_SAFE, ALU.max, ALU.min, r_, w_)
                k.act(S_(nm)[:], sm["r_" + nm][:], AF.Sin, r_, w_)
            k.tt(V, LR[:, d * 16:(d + 1) * 16], sm["mag"][:], sm["cos"][:], ALU.mult, r_ + [btab], [btab])
            k.tt(V, LI[:, d * 16:(d + 1) * 16], sm["mag"][:], sm["sin"][:], ALU.mult, r_ + [btab], [btab])
            lr = LR[:, d * 16:(d + 1) * 16]; li = LI[:, d * 16:(d + 1) * 16]
            r2 = r_ + [btab]
            k.tt(V, S_("den")[:], sm["a_re"][:], sm["a_re"][:], ALU.mult, r2, w_)
            k.tt(V, S_("den2")[:], sm["a_im"][:], sm["a_im"][:], ALU.mult, r2, w_)
            k.tt(V, sm["den"][:], sm["den"][:], sm["den2"][:], ALU.add, r2, w_)
            k.recip(S_("rden")[:], sm["den"][:], r2, w_)
            k.ts(V, S_("nr")[:], lr, -1.0, None, ALU.add, None, r2, w_)
            k.tt(V, S_("p1")[:], sm["nr"][:], sm["a_re"][:], ALU.mult, r2, w_)
            k.tt(V, S_("p2")[:], li, sm["a_im"][:], ALU.mult, r2, w_)
            k.tt(V, sm["p1"][:], sm["p1"][:], sm["p2"][:], ALU.add, r2, w_)
            k.tt(V, S_("fr")[:], sm["p1"][:], sm["rden"][:], ALU.mult, r2, w_)
            k.tt(V, S_("p3")[:], li, sm["a_re"][:], ALU.mult, r2, w_)
            k.tt(V, S_("p4")[:], sm["nr"][:], sm["a_im"][:], ALU.mult, r2, w_)
            k.tt(V, sm["p3"][:], sm["p3"][:], sm["p4"][:], ALU.subtract, r2, w_)
            k.tt(V, S_("fi")[:], sm["p3"][:], sm["rden"][:], ALU.mult, r2, w_)
            k.memset(V, Ec[:, :, 0:1], 1.0, w_)
            k.memset(V, Es[:, :, 0:1], 0.0, w_)
            k.cp(V, Ec[:, :, 1:2], sm["cos"][:].unsqueeze(2), r_, w_)
            k.cp(V, Es[:, :, 1:2], sm["sin"][:].unsqueeze(2), r_, w_)
            m = 1
            while m < 127:
                n = min(m, 127 - m)
                cm = Ec[:, :, m:m + 1].to_broadcast([128, 16, n])
                smm = Es[:, :, m:m + 1].to_broadcast([128, 16, n])
                k.tt(V, t1[:, :, 0:n], Ec[:, :, 1:1 + n], cm, ALU.mult, r_, w_)
                k.tt(V, t2[:, :, 0:n], Es[:, :, 1:1 + n], smm, ALU.mult, r_, w_)
                k.tt(V, Ec[:, :, m + 1:m + 1 + n], t1[:, :, 0:n], t2[:, :, 0:n], ALU.subtract, r_, w_)
                k.tt(V, t1[:, :, 0:n], Ec[:, :, 1:1 + n], smm, ALU.mult, r_, w_)
                k.tt(V, t2[:, :, 0:n], Es[:, :, 1:1 + n], cm, ALU.mult, r_, w_)
                k.tt(V, Es[:, :, m + 1:m + 1 + n], t1[:, :, 0:n], t2[:, :, 0:n], ALU.add, r_, w_)
                m += n
            frb = sm["fr"][:].unsqueeze(2).to_broadcast([128, 16, 128])
            fib = sm["fi"][:].unsqueeze(2).to_broadcast([128, 16, 128])
            k.tt(V, t1[:], Ec[:], frb, ALU.mult, r_, w_)
            k.tt(V, t2[:], Es[:], fib, ALU.mult, r_, w_)
            k.tt(V, rr[:], t1[:], t2[:], ALU.add, r_, w_)
            k.tt(V, t1[:], Ec[:], fib, ALU.mult, r_, w_)
            k.tt(V, t2[:], Es[:], frb, ALU.mult, r_, w_)
            k.tt(V, ri[:], t1[:], t2[:], ALU.subtract, r_, w_)
            magb = sm["mag"][:].unsqueeze(2).to_broadcast([128, 16, 127])
            rw = r_ + [btabp]
            k.memset(V, tabp[:, 4, :, :], 0.0, [btabp])
            if d == 0:
                k.cp(V, tabp[:, 0, :, :], rr[:], rw, [btabp])
                k.cp(V, tabp[:, 1, :, :], ri[:], rw, [btabp])
                k.cp(V, tabp[:, 2, :, :], Ec[:], rw, [btabp])
                k.cp(V, tabp[:, 3, :, :], Es[:], rw, [btabp])
                k.cp(V, tabp[:, 4, :, 1:128], magb, rw, [btabp])
            else:
                k.cp(V, tabp[:, 0, :, :], rr[:, :, ::-1], rw, [btabp])
                k.cp(V, tabp[:, 1, :, :], ri[:, :, ::-1], rw, [btabp])
                k.cp(V, tabp[:, 2, :, :], Ec[:, :, ::-1], rw, [btabp])
                k.cp(V, tabp[:, 3, :, :], Es[:, :, ::-1], rw, [btabp])
                k.cp(V, tabp[:, 4, :, 0:127], magb, rw, [btabp])
            P.dma("gpsimd", C.scr_tab[d], tabp[:].rearrange("p a j t -> p (a j t)"), reads=[btabp])
            lrb = lr.unsqueeze(2); lib = li.unsqueeze(2)
            k.memset(V, Pre[:, :, 0:1], 1.0, w_)
            k.memset(V, Pim[:, :, 0:1], 0.0, w_)
            k.cp(V, Pre[:, :, 1:2], lrb, r2, w_)
            k.cp(V, Pim[:, :, 1:2], lib, r2, w_)
            m = 1
            while m < GB - 1:
                n = min(m, GB - 1 - m)
                cm = Pre[:, :, m:m + 1].to_broadcast([128, 16, n])
                smm = Pim[:, :, m:m + 1].to_broadcast([128, 16, n])
                k.tt(V, q1[:, :, 0:n], Pre[:, :, 1:1 + n], cm, ALU.mult, r_, w_)
                k.tt(V, q2[:, :, 0:n], Pim[:, :, 1:1 + n], smm, ALU.mult, r_, w_)
                k.tt(V, Pre[:, :, m + 1:m + 1 + n], q1[:, :, 0:n], q2[:, :, 0:n], ALU.subtract, r_, w_)
                k.tt(V, q1[:, :, 0:n], Pre[:, :, 1:1 + n], smm, ALU.mult, r_, w_)
                k.tt(V, q2[:, :, 0:n], Pim[:, :, 1:1 + n], cm, ALU.mult, r_, w_)
                k.tt(V, Pim[:, :, m + 1:m + 1 + n], q1[:, :, 0:n], q2[:, :, 0:n], ALU.add, r_, w_)
                m += n
            mr = MUr[:, d * 16:(d + 1) * 16]; mi = MUi[:, d * 16:(d + 1) * 16]
            k.tt(V, q1[:, :, 0], Pre[:, :, GB - 1], lr, ALU.mult, r2, w_)
            k.tt(V, q2[:, :, 0], Pim[:, :, GB - 1], li, ALU.mult, r2, w_)
            k.tt(V, mr, q1[:, :, 0], q2[:, :, 0], ALU.subtract, r_, [bsm, btab])
            k.tt(V, q1[:, :, 0], Pre[:, :, GB - 1], li, ALU.mult, r2, w_)
            k.tt(V, q2[:, :, 0], Pim[:, :, GB - 1], lr, ALU.mult, r2, w_)
            k.tt(V, mi, q1[:, :, 0], q2[:, :, 0], ALU.add, r_, [bsm, btab])
            if d == 0:
                k.cp(V, Pbr[:], Pre[:, :, ::-1], r_, w_)
                k.cp(V, Pbi[:], Pim[:, :, ::-1], r_, w_)
            else:
                k.cp(V, Pbr[:], Pre[:], r_, w_)
                k.cp(V, Pbi[:], Pim[:], r_, w_)
            for rix, Pb in enumerate((Pbr, Pbi)):
                for kt in range(GB // 128):
                    for j0 in (0, 8):
                        for jj in range(8):
                            k.tr(plam[:, jj, :], Pb[:, j0 + jj, kt * 128:(kt + 1) * 128], C.identb[:], [bsm, C.bid],
                                 [bplam])
                        k.cp("scalar", lamt[:, rix, kt, j0:j0 + 8, :], plam[:], [bplam], [blam])
            P.dma("gpsimd", C.scr_lam[d], lamt[:].rearrange("p r k j s -> p (r k j s)"), reads=[blam])
            P.dma("sync", bgr[:], C.inp["bg_re"][d], writes=[bsm])
            P.dma("sync", bgi[:], C.inp["bg_im"][d], writes=[bsm])
            frb32 = sm["fr"][:].unsqueeze(2).to_broadcast([128, 16, 32])
            fib32 = sm["fi"][:].unsqueeze(2).to_broadcast([128, 16, 32])
            k.tt(V, q1[:, :, 0:32], bgr[:], frb32, ALU.mult, r_, w_)
            k.tt(V, q2[:, :, 0:32], bgi[:], fib32, ALU.mult, r_, w_)
            k.tt(V, BB[:, d, 0], q1[:, :, 0:32], q2[:, :, 0:32], ALU.subtract, r_, [bBB])
            k.tt(V, q1[:, :, 0:32], bgi[:], frb32, ALU.mult, r_, w_)
            k.tt(V, q2[:, :, 0:32], bgr[:], fib32, ALU.mult, r_, w_)
            k.tt(V, BB[:, d, 1], q1[:, :, 0:32], q2[:, :, 0:32], ALU.add, r_, [bBB])
        P.barrier()
        pst.close()
        Bre = T("Bre", [128, 1, 16, 128], BF16); Bim = T("Bim", [128, 1, 16, 128], BF16)
        Cre = T("Cre", [128, 1, 16, 128], BF16); Cim = T("Cim", [128, 1, 16, 128], BF16)
        bw = Buf()
        wglu = T("wglu", [128, 4, 512], BF16)
        P.dma("gpsimd", wglu[:], C.inp["w_glu"].rearrange("(c p) n -> p c n", p=128), writes=[bw])
        bglu = T("bglu", [128, 4]); Dv = T("Dv", [128, 4])
        P.dma("sync", bglu[:], C.inp["b_glu"][:, :], writes=[bw])
        P.dma("sync", Dv[:], C.inp["s5_d"][:, :], writes=[bw])
        ones = T("ones", [128, 128], BF16)
        k.memset(V, ones[:], 1.0, [bw])
        TB = T("TB", [128, 5, 16, 128])
        Rre, Rim, Ere, Eim, DEC = (TB[:, a_, :, :] for a_ in range(5))
        Eb = T("Eb", [128, 2, 16, 128], BF16); bEb = Buf()
        Gbr = T("Gbr", [128, 16, 128], BF16); Gbi = T("Gbi", [128, 16, 128], BF16); bGb = Buf()
        tC = T("tC", [128, 16, 128], BF16); tD = T("tD", [128, 16, 128], BF16); btp = Buf()
        utr = Ring([T(f"ut{i}", [128, 4, 512], BF16) for i in range(2)])
        BUr = T("BUr", [128, 16, 128], BF16); bBUr = Buf()
        BUi = T("BUi", [128, 16, 128], BF16); bBUi = Buf()
        Rb = T("Rb", [128, 2, 16, 128], BF16); bRb = Buf()
        tB = T("tB", [128, 16, 128], BF16)
        Xre = T("Xre", [128, 16, 128]); bXr = Buf()
        Xim = T("Xim", [128, 16, 128]); bXi = Buf()
        Grr = Ring([T(f"Gre{i}", [128, 16, 128]) for i in range(1)])
        Gir = Ring([T(f"Gim{i}", [128, 16, 128]) for i in range(1)])
        lam = T("lam", [128, 2, 2, 16, 128], BF16); blm = Buf()
        utkr = Ring([T(f"utk{i}", [128, GB // 128, 512], BF16) for i in range(2)])
        w1 = T("w1", [128, 16, 32]); w2 = T("w2", [128, 16, 32]); bw12 = Buf()
        Sre = T("Sre", [128, 16]); Sim = T("Sim", [128, 16]); bS = Buf()
        Hre = T("Hre", [128, 16, 128], BF16); bHr = Buf()
        Him = T("Him", [128, 16, 128], BF16); bHi = Buf()
        tA = T("tA", [128, 16, 128], BF16); btv = Buf()
        cs = T("cs", [128, 16, 4]); bcs = Buf()
        HLre = T("HLre", [128, 16]); HLim = T("HLim", [128, 16]); bHL = Buf()
        ystr = Ring([T(f"yst{i}", [128, 4, 128]) for i in range(1)])
        yflr = Ring([T(f"yfl{i}", [128, 4, 128]) for i in range(2)])
        y2r = Ring([T(f"y2{i}", [128, 4, 128]) for i in range(1)])
        g1 = T("g1", [128, 4, 128]); g2 = T("g2", [128, 4, 128]); bg1 = Buf()
        zr = Ring([T(f"z{i}", [128, 4, 128], BF16) for i in range(1)])
        gate = g2; bgate = bg1
        aa = T("aa", [128, 4, 128]); baa = Buf()
        sqb = T("sqb", [128, 4, 128], BF16); bsqb = Buf()
        rt = T("rt", [128, 128]); brt = Buf()
        anr = Ring([T(f"an{i}", [128, 4, 128], BF16) for i in range(1)])
        psRr = Ring([PS(f"psR{i}", [128, 4, 128]) for i in range(2)])
        psIr = Ring([PS(f"psI{i}", [128, 4, 128]) for i in range(2)])
        psYr = Ring([PS(f"psY{i}", [128, 512]) for i in range(2)])
        pg = PS("pg", [128, 4, 128]); bpg = Buf()
        pss = PS("pss", [128, 512]); bpss = Buf()
        fl = lambda ap: ap.rearrange("p a t -> p (a t)")
        G = "gpsimd"
        for d in range(2):
            dsl = slice(d * 16, (d + 1) * 16)
            P.dma("sync", TB[:].rearrange("p a j t -> p (a j t)"), C.scr_tab[d], writes=[btab])
            P.dma("gpsimd", Bre[:, 0], C.inp["bl_re"][d], writes=[bw])
            P.dma("gpsimd", Bim[:, 0], C.inp["bl_im"][d], writes=[bw])
            P.dma("gpsimd", Cre[:, 0], C.inp["cl_re"][d], writes=[bw])
            P.dma("gpsimd", Cim[:, 0], C.inp["cl_im"][d], writes=[bw])
            k.ts("gpsimd", Cim[:], Cim[:], -1.0, 1.0, ALU.mult, ALU.mult, [bw], [bw])
            k.cp("scalar", Eb[:, 0], Ere, [btab], [bEb])
            k.cp("scalar", Eb[:, 1], Eim, [btab], [bEb])
            k.cp("scalar", Rb[:, 0], Rre, [btab], [bRb])
            k.cp("scalar", Rb[:, 1], Rim, [btab], [bRb])
            for dom in DOMS:
                S = C.scr[dom["name"]]
                NP, p_lo = dom["NP"], dom["p_lo"]
                UTv = S["UT"].rearrange("(c p) e -> p c e", p=128)
                YFv = S["YF"].rearrange("(c p) e -> p c e", p=128)
                ANv = S["AN"].rearrange("(c p) e -> p c e", p=128)
                npc = NP // 128
                k.memset(V, HLre[:], 0.0, [bHL])
                k.memset(V, HLim[:], 0.0, [bHL])
                order = list(range(npc)) if d == 0 else list(range(npc - 1, -1, -1))
                items = []
                if dom["carry"]:
                    P.dma("sync", lam[:].rearrange("p r k j s -> p (r k j s)"), C.scr_lam[d], writes=[blm])
                    UTK = C.scr_pf["UTPF" if d == 0 else "UTSF"]
                    blks = list(range(PFX // GB))
                    if d == 1:
                        blks.reverse()
                    mr = MUr[:, dsl]; mi = MUi[:, dsl]
                    for blk in blks:
                        utk, butk = utkr.next()
                        P.dma("sync", utk[:], UTK[blk * GB:(blk + 1) * GB, :].rearrange("(k p) c -> p k c", p=128),
                              writes=[butk])
                        psR, bpsR = psRr.next()
                        psI, bpsI = psIr.next()
                        gps = (psR[:].rearrange("p a t -> p (a t)"), psI[:].rearrange("p a t -> p (a t)"))
                        gbf = (bpsR, bpsI)
                        for j in range(16):
                            for ri in range(2):
                                for kt in range(GB // 128):
                                    k.mm(gps[ri][:, 32 * j:32 * j + 32], lam[:, ri, kt, j, :], utk[:, kt, 32 * j:32 * j + 32],
                                         kt == 0, kt == GB // 128 - 1, [blm, butk], [gbf[ri]])
                        g3 = [g.rearrange("p (j c) -> p j c", c=32) for g in gps]
                        k.tt(V, w1[:], BB[:, d, 0], g3[0], ALU.mult, [bBB, bpsR], [bw12])
                        k.tt(V, w2[:], BB[:, d, 1], g3[1], ALU.mult, [bBB, bpsI], [bw12])
                        k.tt(V, w1[:], w1[:], w2[:], ALU.subtract, [bw12], [bw12])
                        k.red(Sre[:], w1[:], [bw12], [bS])
                        k.tt(V, w1[:], BB[:, d, 0], g3[1], ALU.mult, [bBB, bpsI], [bw12])
                        k.tt(V, w2[:], BB[:, d, 1], g3[0], ALU.mult, [bBB, bpsR], [bw12])
                        k.tt(V, w1[:], w1[:], w2[:], ALU.add, [bw12], [bw12])
                        k.red(Sim[:], w1[:], [bw12], [bS])
                        k.tt(V, cs[:, :, 0], mr, HLre[:], ALU.mult, [btab, bHL], [bcs])
                        k.tt(V, cs[:, :, 1], mi, HLim[:], ALU.mult, [btab, bHL], [bcs])
                        k.tt(V, cs[:, :, 2], mr, HLim[:], ALU.mult, [btab, bHL], [bcs])
                        k.tt(V, cs[:, :, 3], mi, HLre[:], ALU.mult, [btab, bHL], [bcs])
                        k.tt(V, HLre[:], cs[:, :, 0], cs[:, :, 1], ALU.subtract, [bcs], [bHL])
                        k.tt(V, HLre[:], HLre[:], Sre[:], ALU.add, [bS, bHL], [bHL])
                        k.tt(V, HLim[:], cs[:, :, 2], cs[:, :, 3], ALU.add, [bcs], [bHL])
                        k.tt(V, HLim[:], HLim[:], Sim[:], ALU.add, [bS, bHL], [bHL])
                items += [("main", pi_) for pi_ in order]
                cur_chunk = None
                for kind, pi in items:
                    ci = pi // 4
                    if (kind, ci) != cur_chunk:
                        ut, but = utr.next()
                        P.dma("sync", ut[:], UTv[:, :, p_lo + ci * 512:p_lo + ci * 512 + 512], writes=[but])
                        cur_chunk = (kind, ci)
                    col = (pi % 4) * 128
                    if kind == "pre":
                        pass
                    elif d == 0:
                        yst, byst = ystr.next()
                    else:
                        yfl, byfl = yflr.next()
                        P.dma("sync", yfl[:], YFv[:, :, pi * 128:(pi + 1) * 128], writes=[byfl])
                        y2, by2 = y2r.next()
                    for b in range(4):
                        psR, bpsR = psRr.next()
                        psI, bpsI = psIr.next()
                        for i in range(4):
                            j = 4 * b + i
                            k.mm(psR[:, i, :], Bre[:, 0, j, :], ut[:, b, col:col + 128], True, True, [bw, but], [bpsR])
                            k.mm(psI[:, i, :], Bim[:, 0, j, :], ut[:, b, col:col + 128], True, True, [bw, but], [bpsI])
                        k.cp("scalar", BUr[:, 4 * b:4 * b + 4, :], psR[:], [bpsR], [bBUr])
                        k.cp("scalar", BUi[:, 4 * b:4 * b + 4, :], psI[:], [bpsI], [bBUi])
                    k.tt(V, tA[:], Rb[:, 1], BUi[:], ALU.mult, [bRb, bBUi], [btv])
                    k.tt(V, tB[:], Rb[:, 0], BUr[:], ALU.mult, [bRb, bBUr], [btv])
                    k.tt(V, Xre[:], tB[:], tA[:], ALU.subtract, [btv], [bXr])
                    k.tt(V, tA[:], Rb[:, 0], BUi[:], ALU.mult, [bRb, bBUi], [btv])
                    k.tt(V, tB[:], Rb[:, 1], BUr[:], ALU.mult, [bRb, bBUr], [btv])
                    k.tt(V, Xim[:], tA[:], tB[:], ALU.add, [btv], [bXi])
                    c0 = 0 if d == 0 else 127
                    k.tt(V, cs[:, :, 0], LR[:, dsl], HLre[:], ALU.mult, [btab, bHL], [bcs])
                    k.tt(V, cs[:, :, 1], LI[:, dsl], HLim[:], ALU.mult, [btab, bHL], [bcs])
                    k.tt(V, cs[:, :, 2], LR[:, dsl], HLim[:], ALU.mult, [btab, bHL], [bcs])
                    k.tt(V, cs[:, :, 3], LI[:, dsl], HLre[:], ALU.mult, [btab, bHL], [bcs])
                    k.tt(V, Xre[:, :, c0], Xre[:, :, c0], cs[:, :, 0], ALU.add, [bcs, bXr], [bXr])
                    k.tt(V, Xre[:, :, c0], Xre[:, :, c0], cs[:, :, 1], ALU.subtract, [bcs, bXr], [bXr])
                    k.tt(V, Xim[:, :, c0], Xim[:, :, c0], cs[:, :, 2], ALU.add, [bcs, bXi], [bXi])
                    k.tt(V, Xim[:, :, c0], Xim[:, :, c0], cs[:, :, 3], ALU.add, [bcs, bXi], [bXi])
                    Gre, bGr = Grr.next()
                    Gim, bGi = Gir.next()
                    if d == 0:
                        k.scan(fl(Gre[:]), fl(DEC), fl(Xre[:]), [btab, bXr], [bGr])
                        k.scan(fl(Gim[:]), fl(DEC), fl(Xim[:]), [btab, bXi], [bGi])
                    else:
                        k.scan(fl(Gre[:])[:, ::-1], fl(DEC)[:, ::-1], fl(Xre[:])[:, ::-1], [btab, bXr], [bGr])
                        k.scan(fl(Gim[:])[:, ::-1], fl(DEC)[:, ::-1], fl(Xim[:])[:, ::-1], [btab, bXi], [bGi])
                    c1 = 127 if d == 0 else 0
                    k.tt(V, cs[:, :, 0], Ere[:, :, c1], Gre[:, :, c1], ALU.mult, [btab, bGr], [bcs])
                    k.tt(V, cs[:, :, 1], Eim[:, :, c1], Gim[:, :, c1], ALU.mult, [btab, bGi], [bcs])
                    k.tt(V, cs[:, :, 2], Eim[:, :, c1], Gre[:, :, c1], ALU.mult, [btab, bGr], [bcs])
                    k.tt(V, cs[:, :, 3], Ere[:, :, c1], Gim[:, :, c1], ALU.mult, [btab, bGi], [bcs])
                    k.tt(V, HLre[:], cs[:, :, 0], cs[:, :, 1], ALU.subtract, [bcs], [bHL])
                    k.tt(V, HLim[:], cs[:, :, 2], cs[:, :, 3], ALU.add, [bcs], [bHL])
                    if kind == "pre":
                        continue
                    k.cp("scalar", Gbr[:], Gre[:], [bGr], [bGb])
                    k.cp("scalar", Gbi[:], Gim[:], [bGi], [bGb])
                    k.tt(V, tC[:], Eb[:, 0], Gbr[:], ALU.mult, [bEb, bGb], [btp])
                    k.tt(V, tD[:], Eb[:, 1], Gbi[:], ALU.mult, [bEb, bGb], [btp])
                    k.tt(V, Hre[:], tC[:], tD[:], ALU.subtract, [btp], [bHr])
                    k.tt(V, tC[:], Eb[:, 1], Gbr[:], ALU.mult, [bEb, bGb], [btp])
                    k.tt(V, tD[:], Eb[:, 0], Gbi[:], ALU.mult, [bEb, bGb], [btp])
                    k.tt(V, Him[:], tC[:], tD[:], ALU.add, [btp], [bHi])
                    for b in range(4):
                        psY, bpsY = psYr.next()
                        for i in range(4):
                            j = 4 * b + i
                            k.mm(psY[:, 0:128], Cre[:, 0, j, :], Hre[:, j, :], i == 0, False, [bw, bHr], [bpsY])
                            k.mm(psY[:, 0:128], Cim[:, 0, j, :], Him[:, j, :], False, i == 3, [bw, bHi], [bpsY])
                        if d == 0:
                            k.cp("scalar", yst[:, b, :], psY[:, 0:128], [bpsY], [byst])
                        else:
                            k.tt(V, y2[:, b, :], psY[:, 0:128], yfl[:, b, :], ALU.add, [bpsY, byfl], [by2])
                            k.stt(y2[:, b, :], ut[:, b, col:col + 128], Dv[:, b:b + 1], y2[:, b, :], ALU.mult, ALU.add,
                                  [but, bw, by2], [by2])
                    if d == 0:
                        P.dma("gpsimd", YFv[:, :, pi * 128:(pi + 1) * 128], yst[:], reads=[byst])
                        continue
                    G = "gpsimd"
                    k.tt(G, g1[:], y2[:], y2[:], ALU.mult, [by2], [bg1])
                    k.ts(G, g1[:], g1[:], 0.044715, 1.0, ALU.mult, ALU.add, [bg1], [bg1])
                    k.tt(G, g1[:], g1[:], y2[:], ALU.mult, [bg1, by2], [bg1])
                    k.act(g2[:], g1[:], AF.Sigmoid, [bg1], [bg1], scale=2.0 * math.sqrt(2.0 / math.pi))
                    z, bz = zr.next()
                    k.tt(G, z[:], y2[:], g2[:], ALU.mult, [by2, bg1], [bz])
                    for co in range(4):
                        for ci_ in range(4):
                            k.mm(pg[:, co, :], wglu[:, ci_, co * 128:(co + 1) * 128], z[:, ci_, :], ci_ == 0, ci_ == 3,
                                 [bw, bz], [bpg])
                    for co in range(4):
                        k.act(gate[:, co, :], pg[:, co, :], AF.Sigmoid, [bpg, bw], [bgate], bias=bglu[:, co:co + 1])
                    k.tt(G, aa[:], z[:], gate[:], ALU.mult, [bz, bgate], [baa])
                    k.tt(G, sqb[:], aa[:], aa[:], ALU.mult, [baa], [bsqb])
                    for c in range(4):
                        k.mm(pss[:, 0:128], ones[:], sqb[:, c, :], c == 0, c == 3, [bw, bsqb], [bpss])
                    k.act(rt[:], pss[:, 0:128], AF.Sqrt, [bpss], [brt], scale=1.0 / 512, bias=EPS)
                    k.recip(rt[:], rt[:], [brt], [brt])
                    an, ban = anr.next()
                    k.tt(V, an[:], aa[:], rt[:].unsqueeze(1).to_broadcast([128, 4, 128]), ALU.mult, [baa, brt], [ban])
                    P.dma("gpsimd", ANv[:, :, pi * 128:(pi + 1) * 128], an[:], reads=[ban])
            P.barrier()


def phase_comb(C):
    nc, P, k = C.nc, C.P, C.k
    with ExitStack() as st:
        T = lambda n, s, d=F32: st.enter_context(nc.sbuf_tensor("cb_" + n, s, d))
        PS = lambda n, s, d=F32: st.enter_context(nc.psum_tensor("cb_" + n, s, d))
        V, G = "vector", "gpsimd"
        wout = T("wout", [128, 8, 1024], BF16); bwo = Buf()
        go = T("go", [128, 8]); bgo = Buf()
        P.dma("sync", go[:], C.inp["g_out"][:, :], writes=[bgo])
        wst = Ring([T(f"wst{i}", [128, 1024]) for i in range(2)])
        for kk in range(8):
            t, b = wst.next()
            P.dma("sync", t[:], C.inp["w_out"][kk * 128:(kk + 1) * 128, :], writes=[b])
            k.ts(V, wout[:, kk, :], t[:], go[:, kk:kk + 1], None, ALU.mult, None, [b, bgo], [bwo])
        zt = T("zt", [128, 1024]); bzt = Buf()
        k.memset(V, zt[:], 0.0, [bzt])
        o1r = Ring([T(f"o1_{i}", [128, 8, 65]) for i in range(2)])
        o4r = Ring([T(f"o4_{i}", [128, 8, 65]) for i in range(2)])
        o16r = Ring([T(f"o16_{i}", [128, 8, 65]) for i in range(2)])
        xr = Ring([T(f"x{i}", [128, 1024]) for i in range(3)])
        vr = Ring([T(f"vl{i}", [128, 1]) for i in range(3)])
        anr = Ring([T(f"an{i}", [128, 4, 512], BF16) for i in range(2)])
        dnr = Ring([T(f"dn{i}", [128, 16]) for i in range(2)])
        bfr = Ring([T(f"bf{i}", [128, 8, 64]) for i in range(2)])
        junk = T("junk", [128, 512], BF16); bjunk = Buf()
        ssr = Ring([T(f"ss{i}", [128, 4]) for i in range(3)])
        bnr = Ring([T(f"bn{i}", [128, 512], BF16) for i in range(2)])
        bntr = Ring([T(f"bnT{i}", [128, 4, 128], BF16) for i in range(2)])
        x1r = Ring([T(f"x1_{i}", [128, 1024]) for i in range(2)])
        ptb_ = PS("ptb", [128, 8, 128], BF16); bptb = Buf()
        ptb = ptb_[:, 0:4, :]
        pxr = Ring([PS(f"px{i}", [128, 512]) for i in range(4)])
        identb, bid = C.identb, C.bid
        for dom in DOMS:
            S = C.scr[dom["name"]]
            NP, p_lo, N, ext = dom["NP"], dom["p_lo"], dom["N"], dom["ext"]
            xsrc = C.inp["x_" + dom["name"]]
            vsrc = C.inp["valid_s" if dom["name"] == "s" else "valid_p"]
            ANv = S["AN"].rearrange("(c p) e -> p c e", p=128)
            if ext == 0:
                P.dma("gpsimd", S["X1"][0:128, :], zt[:], reads=[bzt])
                P.dma("gpsimd", S["X1"][128 + N:256 + N, :], zt[:], reads=[bzt])
            for ti in range(NP // 128):
                if ti % 4 == 0:
                    anc, banc = anr.next()
                    wcn = min(512, NP - ti * 128)
                    P.dma("sync", anc[:, :, 0:wcn], ANv[:, :, ti * 128:ti * 128 + wcn], writes=[banc])
                col = (ti % 4) * 128
                o1, bo1 = o1r.next(); o4, bo4 = o4r.next(); o16, bo16 = o16r.next()
                rs = slice(ti * 128, ti * 128 + 128)
                P.dma("sync", o1[:].rearrange("p h e -> p (h e)"), S["O1"][rs, :], writes=[bo1])
                P.dma("sync", o4[:].rearrange("p h e -> p (h e)"), S["O4"][rs, :], writes=[bo4])
                P.dma("sync", o16[:].rearrange("p h e -> p (h e)"), S["O16"][rs, :], writes=[bo16])
                xt, bx = xr.next()
                er = p_lo + ti * 128
                P.dma("sync", xt[:], xsrc[er - dom["comp_lo"]:er - dom["comp_lo"] + 128, :], writes=[bx])
                vt, bvt = vr.next()
                P.dma("sync", vt[:], vsrc[er:er + 128, :], writes=[bvt])
                k.tt(V, o1[:], o1[:], o4[:], ALU.add, [bo1, bo4], [bo1])
                k.tt(V, o1[:], o1[:], o16[:], ALU.add, [bo1, bo16], [bo1])
                dn, bdn = dnr.next()
                k.ts(V, dn[:, 0:8], o1[:, :, 64], 1e-30, None, ALU.max, None, [bo1], [bdn])
                k.recip(dn[:, 8:16], dn[:, 0:8], [bdn], [bdn])
                bfp, bbf = bfr.next()
                k.tt(V, bfp[:], o1[:, :, 0:64], dn[:, 8:16].unsqueeze(2).to_broadcast([128, 8, 64]), ALU.mult,
                     [bo1, bdn], [bbf])
                ss, bss = ssr.next()
                k.act(junk[:], bfp[:].rearrange("p h e -> p (h e)"), AF.Square, [bbf], [bjunk, bss], accum=ss[:, 0:1])
                k.act(ss[:, 1:2], ss[:, 0:1], AF.Sqrt, [bss], [bss], scale=1.0 / 512, bias=EPS)
                k.recip(ss[:, 2:3], ss[:, 1:2], [bss], [bss])
                bn, bbn = bnr.next()
                k.act(bn[:], bfp[:].rearrange("p h e -> p (h e)"), AF.Copy, [bbf, bss], [bbn], scale=ss[:, 2:3])
                for c4 in range(4):
                    k.tr(ptb[:, c4, :], bn[:, c4 * 128:(c4 + 1) * 128], identb[:], [bbn, bid], [bptb])
                bnT, bbnT = bntr.next()
                k.cp(V, bnT[:], ptb, [bptb], [bbnT])
                x1t, bx1 = x1r.next()
                for half in range(2):
                    px, bpx = pxr.next()
                    k.mm_group([(px[:], anc[:, kk, col:col + 128] if kk < 4 else bnT[:, kk - 4, :],
                                 wout[:, kk, half * 512:(half + 1) * 512], kk == 0, kk == 7) for kk in range(8)],
                               [banc, bbnT, bwo], [bpx])
                    k.stt(x1t[:, half * 512:(half + 1) * 512], px[:], vt[:, 0:1], xt[:, half * 512:(half + 1) * 512],
                          ALU.mult, ALU.add, [bpx, bvt, bx], [bx1])
                P.dma("gpsimd", S["X1"][128 - ext + ti * 128:128 - ext + ti * 128 + 128, :], x1t[:], reads=[bx1])
    P.barrier()


def phase_ffn(C):
    nc, P, k = C.nc, C.P, C.k
    with ExitStack() as st:
        T = lambda n, s, d=F32: st.enter_context(nc.sbuf_tensor("ff_" + n, s, d))
        PS = lambda n, s, d=F32: st.enter_context(nc.psum_tensor("ff_" + n, s, d))
        V, G = "vector", "gpsimd"
        wup = T("wup", [128, 8, 2 * DFF], BF16); bwu = Buf()
        wdn = T("wdn", [128, NFC, 1024], BF16); bwd = Buf()
        gf = T("gf", [128, 8]); bgf = Buf()
        P.dma("sync", gf[:], C.inp["g_ffn"][:, :], writes=[bgf])
        pst = ExitStack()
        wst = Ring([pst.enter_context(nc.sbuf_tensor(f"ff_wst{i}", [128, 1408], F32)) for i in range(2)])
        for kk in range(8):
            for q4 in range(4):
                t, b = wst.next()
                P.dma("sync", t[:], C.inp["w_up"][kk * 128:(kk + 1) * 128, q4 * 1408:(q4 + 1) * 1408], writes=[b])
                k.ts(V, wup[:, kk, q4 * 1408:(q4 + 1) * 1408], t[:], gf[:, kk:kk + 1], None, ALU.mult, None,
                     [b, bgf], [bwu])
        P.barrier()
        pst.close()
        wdv = C.inp["w_down"].rearrange("(c p) n -> p c n", p=128)
        for c0 in range(0, NFC, 2):
            P.dma("gpsimd", wdn[:, c0:c0 + 2, :], wdv[:, c0:c0 + 2, :], writes=[bwd])
        cw = T("cw", [128, 44, 3]); cb = T("cbias", [128, 44]); bcw = Buf()
        P.dma("sync", cw[:], C.inp["conv_w"][:, :, :], writes=[bcw])
        P.dma("sync", cb[:], C.inp["conv_b"][:, :], writes=[bcw])
        xr = Ring([T(f"x{i}", [128, 1024]) for i in range(2)])
        junk = T("junk", [128, 1024], BF16); bjunk = Buf()
        ssr = Ring([T(f"ss{i}", [128, 4]) for i in range(3)])
        nbr = Ring([T(f"nb{i}", [128, 1024], BF16) for i in range(2)])
        n2T = T("n2T", [128, 8, 512], BF16); bn2T = Buf()
        actT = T("actT", [128, NFC, 512], BF16); bact = Buf()
        cgr = Ring([T(f"cg{i}", [128, 512]) for i in range(2)])
        cur = Ring([T(f"cu{i}", [128, 512]) for i in range(2)])
        sgr = Ring([T(f"sg{i}", [128, 512]) for i in range(2)])
        x2r = Ring([T(f"x2_{i}", [128, 1024]) for i in range(2)])
        ytr = Ring([T(f"yt{i}", [128, 1024]) for i in range(2)])
        ptr = PS("ptr", [128, 8, 128], BF16); bptr = Buf()
        phr = Ring([PS(f"ph{i}", [128, 512]) for i in range(4)])
        pyr = Ring([PS(f"py{i}", [128, 512]) for i in range(2)])
        identb, bid = C.identb, C.bid
        for dom in DOMS:
            S = C.scr[dom["name"]]
            N = dom["N"]
            yout = C.out[dom["name"]]
            X1 = S["X1"]
            for b0 in range(0, N, 510):
                BT = min(510, N - b0)
                NB = BT + 2
                for i in range((NB + 127) // 128):
                    rows = min(128, NB - 128 * i)
                    r0 = 127 + b0 + 128 * i
                    xt, bx = xr.next()
                    P.dma("sync", xt[0:rows, :], X1[r0:r0 + rows, :], writes=[bx])
                    ss, bss = ssr.next()
                    k.act(junk[0:rows, :], xt[0:rows, :], AF.Square, [bx], [bjunk, bss], accum=ss[0:rows, 0:1])
                    k.act(ss[0:rows, 1:2], ss[0:rows, 0:1], AF.Sqrt, [bss], [bss], scale=1.0 / D, bias=EPS)
                    k.recip(ss[0:rows, 2:3], ss[0:rows, 1:2], [bss], [bss])
                    nbt, bnb = nbr.next()
                    k.act(nbt[0:rows, :], xt[0:rows, :], AF.Copy, [bx, bss], [bnb], scale=ss[0:rows, 2:3])
                    for kk in range(8):
                        k.tr(ptr[:, kk, 0:rows], nbt[0:rows, kk * 128:(kk + 1) * 128], identb[0:rows, 0:rows],
                             [bnb, bid], [bptr])
                    k.cp(V, n2T[:, :, 128 * i:128 * i + rows], ptr[:, :, 0:rows], [bptr], [bn2T])
                for j in range(NFC):
                    outs = []
                    for hf in range(2):
                        ch = j + NFC * hf
                        ph, bph = phr.next()
                        k.mm_group([(ph[:, 0:NB], wup[:, kk, ch * 128:(ch + 1) * 128], n2T[:, kk, 0:NB], kk == 0, kk == 7)
                                    for kk in range(8)], [bwu, bn2T], [bph])
                        cg, bcg = (cgr if hf == 0 else cur).next()
                        k.act(cg[:, 0:BT], ph[:, 0:BT], AF.Identity, [bph, bcw], [bcg], scale=cw[:, ch, 0:1],
                              bias=cb[:, ch:ch + 1])
                        k.stt(cg[:, 0:BT], ph[:, 1:BT + 1], cw[:, ch, 1:2], cg[:, 0:BT], ALU.mult, ALU.add,
                              [bph, bcw, bcg], [bcg])
                        k.stt(cg[:, 0:BT], ph[:, 2:BT + 2], cw[:, ch, 2:3], cg[:, 0:BT], ALU.mult, ALU.add,
                              [bph, bcw, bcg], [bcg])
                        outs.append((cg, bcg))
                    sg, bsg = sgr.next()
                    k.act(sg[:, 0:BT], outs[0][0][:, 0:BT], AF.Silu, [outs[0][1]], [bsg])
                    k.tt(G, actT[:, j, 0:BT], sg[:, 0:BT], outs[1][0][:, 0:BT], ALU.mult, [bsg, outs[1][1]], [bact])
                for io in range((BT + 127) // 128):
                    rows = min(128, BT - 128 * io)
                    x2, bx2 = x2r.next()
                    P.dma("sync", x2[0:rows, :], X1[128 + b0 + 128 * io:128 + b0 + 128 * io + rows, :], writes=[bx2])
                    yt, byt = ytr.next()
                    for half in range(2):
                        py, bpy = pyr.next()
                        k.mm_group([(py[0:rows, :], actT[:, j, 128 * io:128 * io + rows],
                                     wdn[:, j, half * 512:(half + 1) * 512], j == 0, j == NFC - 1) for j in range(NFC)],
                                   [bact, bwd], [bpy])
                        k.tt(V, yt[0:rows, half * 512:(half + 1) * 512], py[0:rows, :],
                             x2[0:rows, half * 512:(half + 1) * 512], ALU.add, [bpy, bx2], [byt])
                    P.dma("gpsimd", yout[b0 + 128 * io:b0 + 128 * io + rows, :], yt[0:rows, :], reads=[byt],
                          is_output=True)
    P.barrier()
```

```python
import math
import numpy as np
from contextlib import ExitStack
import concourse.bass as bass
import concourse.mybir as mybir
from concourse.bass_utils import run_bass_kernel_spmd

F32 = mybir.dt.float32
BF16 = mybir.dt.bfloat16
I32 = mybir.dt.int32
ALU = mybir.AluOpType
AF = mybir.ActivationFunctionType
AX = mybir.AxisListType

NCORES = 8
D = 1024
DFF = 2816
NFC = DFF // 128
HALO = 1152
EPS = 1e-6
T5 = 128
PFX = 12288
GB = 256
DOMS = [
    dict(name="p0", N=2048, ext=0, comp_lo=HALO, comp_hi=HALO + 2048, carry=False),
    dict(name="p1", N=2048, ext=0, comp_lo=HALO, comp_hi=HALO + 2048, carry=False),
    dict(name="s", N=4096, ext=128, comp_lo=0, comp_hi=4096 + 2 * HALO, carry=True),
]
for _d in DOMS:
    _d["NE"] = _d["N"] + 2 * HALO
    _d["p_lo"] = HALO - _d["ext"]
    _d["NP"] = _d["N"] + 2 * _d["ext"]

DEBUG = {"scratch_out": False, "phases": None, "doms": None}
ALL_DOMS = None

EPOCH = 20000
ENGS = ["tensor", "vector", "scalar", "gpsimd", "sync"]
COMPUTE = ["tensor", "vector", "scalar", "gpsimd"]


def sl(start, n, step=1):
    return slice(start, start + (n - 1) * step + 1, step)


class Buf:
    __slots__ = ("name", "w", "r")

    def __init__(self, name=""):
        self.name = name
        self.w = None
        self.r = {}


class Prog:
    def __init__(self, nc, stack, n_eng_sems=5, n_dma_sems=16, same_engine_sync=True):
        self.nc = nc
        self.same_engine_sync = same_engine_sync
        self.ops = {e: [] for e in ENGS}
        self.cnt = {e: 0 for e in COMPUTE}
        self.esems = {e: [stack.enter_context(nc.semaphore(f"s_{e}_{i}")) for i in range(n_eng_sems)]
                      for e in COMPUTE}
        self.dsems = {q: [stack.enter_context(nc.semaphore(f"d_{q}_{i}")) for i in range(n_dma_sems)]
                      for q in ("sync", "gpsimd")}
        self.dval = {q: [0] * n_dma_sems for q in ("sync", "gpsimd")}
        self.dnext = {q: 0 for q in ("sync", "gpsimd")}
        self.known = {e: {} for e in ENGS}
        self.latest = {}
        self.out_events = []

    def _deps(self, reads, writes):
        deps = []
        for b in reads:
            if b.w is not None:
                deps.append(b.w + (True,))
        for b in writes:
            if b.w is not None:
                deps.append(b.w + (False,))
            deps.extend(ev + (False,) for ev in b.r.values())
        return deps

    def _waits(self, eng, deps):
        ws = {}
        kn = self.known[eng]
        for dep in deps:
            s, v, e = dep[0], dep[1], dep[2]
            raw = dep[3] if len(dep) > 3 else True
            if e == eng:
                if eng == "tensor" or not raw or not self.same_engine_sync:
                    continue
            key = id(s)
            if kn.get(key, 0) >= v:
                continue
            if key not in ws or ws[key][1] < v:
                ws[key] = (s, v)
        for key, (s, v) in ws.items():
            kn[key] = v
        return list(ws.values())

    def _commit(self, ev, reads, writes):
        self.latest[id(ev[0])] = ev
        for b in reads:
            old = b.r.get(id(ev[0]))
            if old is None or old[1] < ev[1]:
                b.r[id(ev[0])] = ev
        for b in writes:
            b.w = ev
            b.r = {}

    def op(self, eng, fn, reads=(), writes=()):
        deps = self._deps(reads, writes)
        waits = self._waits(eng, deps)
        self.cnt[eng] += 1
        c = self.cnt[eng]
        s = self.esems[eng][(c - 1) // EPOCH]
        v = (c - 1) % EPOCH + 1
        ev = (s, v, eng)
        self.ops[eng].append((fn, waits, (s, 1)))
        self._commit(ev, reads, writes)
        return ev

    def op_group(self, eng, fns, reads=(), writes=()):
        deps = self._deps(reads, writes)
        waits = self._waits(eng, deps)
        self.cnt[eng] += 1
        c = self.cnt[eng]
        s = self.esems[eng][(c - 1) // EPOCH]
        v = (c - 1) % EPOCH + 1
        ev = (s, v, eng)
        n = len(fns)
        for i, fn in enumerate(fns):
            self.ops[eng].append((fn, waits if i == 0 else [], (s, 1) if i == n - 1 else None))
        self._commit(ev, reads, writes)
        return ev

    def dma(self, q, out, in_, reads=(), writes=(), is_output=False, **kw):
        deps = self._deps(reads, writes)
        i = self.dnext[q]
        self.dnext[q] = (i + 1) % len(self.dsems[q])
        s = self.dsems[q][i]
        if self.dval[q][i] > 0:
            deps.append((s, self.dval[q][i], "dma", True))
        waits = self._waits(q, deps)
        self.dval[q][i] += 16
        ev = (s, self.dval[q][i], "dma")
        self.ops[q].append((lambda e: e.dma_start(out=out, in_=in_, **kw), waits, (s, 16)))
        self._commit(ev, reads, writes)
        if is_output:
            self.out_events.append(ev)
        return ev

    def barrier(self):
        evs = list(self.latest.values())
        for eng in ENGS:
            waits = self._waits(eng, [(s, v, "x") for (s, v, e) in evs])
            if waits:
                self.ops[eng].append((None, waits, None))

    def finish(self):
        self.barrier()

    def emit(self):
        nc = self.nc
        with nc.Block() as block:
            def run(e, lst):
                for fn, waits, inc in lst:
                    for (s, v) in waits:
                        e.wait_ge(s, v)
                    if fn is not None:
                        ins = fn(e)
                        if inc is not None:
                            ins.then_inc(inc[0], inc[1])

            @block.tensor
            def _(e):
                run(e, self.ops["tensor"])

            @block.vector
            def _(e):
                run(e, self.ops["vector"])

            @block.scalar
            def _(e):
                run(e, self.ops["scalar"])

            @block.gpsimd
            def _(e):
                run(e, self.ops["gpsimd"])

            @block.sync
            def _(e):
                run(e, self.ops["sync"])


class K:
    def __init__(self, P):
        self.P = P

    def mm(self, out, lhsT, rhs, start, stop, r, w):
        return self.P.op("tensor", lambda e: e.matmul(out, lhsT=lhsT, rhs=rhs, start=start, stop=stop,
                                                      skip_group_check=True), r, w)

    def mm_group(self, items, r, w):
        fns = [(lambda e, o=o, l=l, rh=rh, st=st, sp=sp: e.matmul(o, lhsT=l, rhs=rh, start=st, stop=sp,
                                                                   skip_group_check=True))
               for (o, l, rh, st, sp) in items]
        return self.P.op_group("tensor", fns, r, w)

    def tr(self, out, in_, ident, r, w):
        return self.P.op("tensor", lambda e: e.transpose(out, in_, ident), r, w)

    def act(self, out, in_, func, r, w, scale=1.0, bias=0.0, accum=None):
        if accum is None:
            return self.P.op("scalar", lambda e: e.activation(out=out, in_=in_, func=func, scale=scale, bias=bias), r, w)
        return self.P.op("scalar", lambda e: e.activation(out=out, in_=in_, func=func, scale=scale, bias=bias,
                                                          accum_out=accum), r, w)

    def tt(self, eng, out, in0, in1, op, r, w):
        return self.P.op(eng, lambda e: e.tensor_tensor(out=out, in0=in0, in1=in1, op=op), r, w)

    def ts(self, eng, out, in0, s1, s2, op0, op1, r, w):
        if op1 is None:
            return self.P.op(eng, lambda e: e.tensor_scalar(out=out, in0=in0, scalar1=s1, scalar2=None, op0=op0), r, w)
        return self.P.op(eng, lambda e: e.tensor_scalar(out=out, in0=in0, scalar1=s1, scalar2=s2, op0=op0, op1=op1), r, w)

    def stt(self, out, in0, scalar, in1, op0, op1, r, w):
        return self.P.op("vector", lambda e: e.scalar_tensor_tensor(out=out, in0=in0, scalar=scalar, in1=in1,
                                                                    op0=op0, op1=op1), r, w)

    def cp(self, eng, out, in_, r, w):
        if eng == "scalar":
            return self.P.op("scalar", lambda e: e.copy(out=out, in_=in_), r, w)
        return self.P.op(eng, lambda e: e.tensor_copy(out=out, in_=in_), r, w)

    def memset(self, eng, ap, val, w):
        return self.P.op(eng, lambda e: e.memset(ap, val), (), w)

    def recip(self, out, in_, r, w):
        return self.P.op("vector", lambda e: e.reciprocal(out=out, in_=in_), r, w)

    def red(self, out, in_, r, w):
        return self.P.op("vector", lambda e: e.tensor_reduce(out=out, in_=in_, axis=AX.X, op=ALU.add), r, w)

    def scan(self, out, d0, d1, r, w):
        return self.P.op("vector", lambda e: e.tensor_tensor_scan(out=out, data0=d0, data1=d1, initial=0.0,
                                                                  op0=ALU.mult, op1=ALU.add), r, w)


class Ring:
    def __init__(self, tiles):
        self.tiles = tiles
        self.bufs = [Buf() for _ in tiles]
        self.i = 0

    def next(self):
        t, b = self.tiles[self.i], self.bufs[self.i]
        self.i = (self.i + 1) % len(self.tiles)
        return t, b


def host_layout(inp):
    f32 = np.float32
    xp = np.asarray(inp["x_prompt"], f32)
    xs = np.asarray(inp["x_sample"], f32)
    shared = {}
    shared["w_in"] = np.ascontiguousarray(inp["w_in"][0], f32)
    shared["w_out"] = np.ascontiguousarray(inp["w_out"][0], f32)
    shared["w_up"] = np.ascontiguousarray(inp["w_up"][0], f32)
    shared["w_down"] = np.ascontiguousarray(inp["w_down"][0], f32)
    shared["w_glu"] = np.ascontiguousarray(inp["w_glu"][0], f32)

    def pk(v, k):
        return np.ascontiguousarray(np.asarray(v, f32).reshape(k, 128).T)

    shared["g_mix"] = pk(inp["norm_mix_g"][0], 8)
    shared["g_ffn"] = pk(inp["norm_ffn_g"][0], 8)
    shared["g_out"] = pk(np.concatenate([inp["ssm_out_g"][0], inp["attn_out_g"][0]]), 8)
    shared["b_glu"] = pk(inp["b_glu"][0], 4)
    shared["s5_d"] = pk(inp["s5_d"][0], 4)
    shared["qg"] = np.ascontiguousarray(np.broadcast_to(np.asarray(inp["q_norm_g"][0], f32)[None, :], (128, 64)))
    shared["kg"] = np.ascontiguousarray(np.broadcast_to(np.asarray(inp["k_norm_g"][0], f32)[None, :], (128, 64)))
    cw = np.asarray(inp["conv_w"][0], f32)
    shared["conv_w"] = np.ascontiguousarray(cw.reshape(3, 44, 128).transpose(2, 1, 0))
    shared["conv_b"] = pk(inp["conv_b"][0], 44)
    def sm(a):
        a = np.asarray(a, f32).reshape(2, 16, 2, 64)
        return np.ascontiguousarray(a.transpose(0, 2, 3, 1).reshape(2, 128, 16))
    shared["a_re"] = sm(inp["s5_a_re"][0])
    shared["a_im"] = sm(inp["s5_a_im"][0])
    ldt = np.broadcast_to(np.asarray(inp["s5_log_dt"][0], f32)[:, :, None], (2, 32, 64))
    shared["log_dt"] = sm(ldt)
    for nm in ("re", "im"):
        b = np.asarray(inp["s5_b_" + nm][0], f32)
        c = np.asarray(inp["s5_c_" + nm][0], f32)
        BL = np.zeros((2, 128, 16, 128), f32)
        CL = np.zeros((2, 128, 16, 128), f32)
        BG = np.zeros((2, 128, 16, 32), f32)
        for j in range(16):
            i = j % 4
            for two in range(2):
                g = 2 * j + two
                r0 = (2 * i + two) * 16
                BL[:, r0:r0 + 16, j, two * 64:(two + 1) * 64] = b[:, g].transpose(0, 2, 1)
                CL[:, two * 64:(two + 1) * 64, j, r0:r0 + 16] = c[:, g].transpose(0, 2, 1)
                BG[:, two * 64:(two + 1) * 64, j, two * 16:(two + 1) * 16] = b[:, g]
        shared["bl_" + nm] = BL
        shared["cl_" + nm] = CL
        shared["bg_" + nm] = BG
    kk = np.arange(128)[:, None]
    qq = np.arange(128)[None, :]
    rel = np.stack([np.abs(128 * m + kk - 64 - qq) for m in range(2)]).astype(f32)
    shared["absrel"] = np.ascontiguousarray(rel.transpose(1, 0, 2))
    shared["val01"] = np.ascontiguousarray((rel <= 64).astype(f32).transpose(1, 0, 2))
    shared["ident"] = np.eye(128, dtype=f32)
    vp = np.zeros((2048 + 2 * HALO, 1), f32)
    vp[HALO:HALO + 2048] = 1.0
    shared["valid_p"] = vp
    maps = []
    for c in range(NCORES):
        m = dict(shared)
        m["x_p0"] = np.ascontiguousarray(xp[2 * c])
        m["x_p1"] = np.ascontiguousarray(xp[2 * c + 1])
        si, q = c // 4, c % 4
        t0 = 4096 * q
        xe = np.zeros((4096 + 2 * HALO, D), f32)
        ve = np.zeros((4096 + 2 * HALO, 1), f32)
        lo, hi = max(t0 - HALO, 0), min(t0 + 4096 + HALO, 16384)
        xe[lo - (t0 - HALO):hi - (t0 - HALO)] = xs[si, lo:hi]
        ve[lo - (t0 - HALO):hi - (t0 - HALO)] = 1.0
        m["x_s"] = xe
        m["valid_s"] = ve
        pf = np.zeros((PFX, D), f32)
        L = max(t0 - 128, 0)
        if L:
            pf[PFX - L:] = xs[si, 0:L]
        sf = np.zeros((PFX, D), f32)
        st_ = t0 + 4096 + 128
        L2 = max(16384 - st_, 0)
        if L2:
            sf[:L2] = xs[si, st_:]
        m["x_pf"] = pf
        m["x_sf"] = sf
        maps.append(m)
    return maps


INPUT_SHAPES = {
    "w_in": [D, 2048], "w_out": [D, D], "w_up": [D, 2 * DFF], "w_down": [DFF, D], "w_glu": [512, 512],
    "g_mix": [128, 8], "g_ffn": [128, 8], "g_out": [128, 8], "b_glu": [128, 4], "s5_d": [128, 4],
    "qg": [128, 64], "kg": [128, 64], "conv_w": [128, 44, 3], "conv_b": [128, 44],
    "a_re": [2, 128, 16], "a_im": [2, 128, 16], "log_dt": [2, 128, 16],
    "bl_re": [2, 128, 16, 128], "bl_im": [2, 128, 16, 128], "cl_re": [2, 128, 16, 128], "cl_im": [2, 128, 16, 128],
    "bg_re": [2, 128, 16, 32], "bg_im": [2, 128, 16, 32],
    "absrel": [128, 2, 128], "val01": [128, 2, 128], "ident": [128, 128],
    "valid_p": [2048 + 2 * HALO, 1], "valid_s": [4096 + 2 * HALO, 1],
    "x_p0": [2048, D], "x_p1": [2048, D], "x_s": [4096 + 2 * HALO, D], "x_pf": [PFX, D], "x_sf": [PFX, D],
}


class Ctx:
    pass


def _h8(ap):
    return ap.rearrange("p (h e) -> p h e", h=8)


def phase_proj(C):
    nc, P, k = C.nc, C.P, C.k
    with ExitStack() as st:
        T = lambda n, s, d=F32: st.enter_context(nc.sbuf_tensor("pj_" + n, s, d))
        PS = lambda n, s, d=F32: st.enter_context(nc.psum_tensor("pj_" + n, s, d))
        win = T("win", [128, 8, 2048], BF16); bwin = Buf()
        gm = T("gm", [128, 8]); bgm = Buf()
        P.dma("sync", gm[:], C.inp["g_mix"][:, :], writes=[bgm])
        wst = Ring([T(f"wst{i}", [128, 2048]) for i in range(2)])
        for kk in range(8):
            t, b = wst.next()
            P.dma("sync", t[:], C.inp["w_in"][kk * 128:(kk + 1) * 128, :], writes=[b])
            k.ts("vector", win[:, kk, :], t[:], gm[:, kk:kk + 1], None, ALU.mult, None, [b, bgm], [bwin])
        qg = T("qg", [128, 64]); bqg = Buf()
        kg = T("kg", [128, 64]); bkg = Buf()
        P.dma("sync", qg[:], C.inp["qg"][:, :], writes=[bqg])
        P.dma("sync", kg[:], C.inp["kg"][:, :], writes=[bkg])
        k.ts("vector", qg[:], qg[:], 0.125, None, ALU.mult, None, [bqg], [bqg])
        zt = T("zt", [128, 4680], BF16); bzt = Buf()
        k.memset("vector", zt[:], 0.0, [bzt])

        xr = Ring([T(f"x{i}", [128, 1024]) for i in range(3)])
        vr = Ring([T(f"vl{i}", [128, 1]) for i in range(3)])
        junk = T("junk", [128, 1024], BF16); bjunk = Buf()
        ssr = Ring([T(f"ss{i}", [128, 4]) for i in range(3)])
        nbr = Ring([T(f"nb{i}", [128, 1024], BF16) for i in range(2)])
        nTs = [T(f"nT{i}", [128, 8, 512], BF16) for i in range(2)]
        nTb = [[Buf() for _ in range(4)] for _ in range(2)]
        sqr = Ring([T(f"sq{i}", [128, 512]) for i in range(2)])
        hsr = Ring([T(f"hs{i}", [128, 24]) for i in range(4)])
        tmr = Ring([T(f"tm{i}", [128, 512]) for i in range(2)])
        qnr = Ring([T(f"qn{i}", [128, 512], BF16) for i in range(2)])
        var = Ring([T(f"va{i}", [128, 8, 65], BF16) for i in range(3)])
        qts = [T(f"qts{i}", [128, 4, 512], BF16) for i in range(2)]; bqts = [Buf(), Buf()]
        kts = [T(f"kts{i}", [128, 4, 512], BF16) for i in range(2)]; bkts = [Buf(), Buf()]
        uts = [T(f"uts{i}", [128, 4, 512], BF16) for i in range(2)]; buts = [Buf(), Buf()]
        ptr = PS("ptr", [128, 8, 128], BF16); bptr = Buf()
        pq = PS("pq", [128, 512]); bpq = Buf()
        pk = PS("pk", [128, 512]); bpk = Buf()
        pv = PS("pv", [128, 512]); bpv = Buf()
        pqt = PS("pqt", [128, 2, 4, 128], BF16); bpqt = [Buf(), Buf()]
        pur = Ring([PS(f"pu{i}", [128, 512]) for i in range(2)])
        identb, bid = C.identb, C.bid
        blk_i = 0
        for dom in DOMS:
            S = C.scr[dom["name"]]
            xsrc = C.inp["x_" + dom["name"]]
            vsrc = C.inp["valid_s" if dom["name"] == "s" else "valid_p"]
            NE = dom["NE"]
            lo, hi = dom["comp_lo"], dom["comp_hi"]
            KTv = S["KT"].rearrange("(c p) e -> p c e", p=128)
            QTv = S["QT"].rearrange("(c p) e -> p c e", p=128)
            UTv = S["UT"].rearrange("(c p) e -> p c e", p=128)
            if lo > 0:
                for (a, b_) in ((0, lo), (hi, NE)):
                    n = b_ - a
                    P.dma("gpsimd", KTv[:, :, a:b_], zt[:, 0:4 * n].rearrange("p (c e) -> p c e", c=4), reads=[bzt])
                    P.dma("gpsimd", S["VA"][a:b_, :].rearrange("(p a) c -> p a c", p=128),
                          zt[:, 0:(n // 128) * 520].rearrange("p (a c) -> p a c", c=520), reads=[bzt])
            e0 = lo
            while e0 < hi:
                nt = min(4, (hi - e0) // 128)
                W = nt * 128
                bi = blk_i % 2
                blk_i += 1
                nTt = nTs[bi]
                for i in range(nt):
                    er = e0 + i * 128
                    xt, bx = xr.next()
                    P.dma("sync", xt[:], xsrc[er - lo:er - lo + 128, :], writes=[bx])
                    vt, bvt = vr.next()
                    P.dma("sync", vt[:], vsrc[er:er + 128, :], writes=[bvt])
                    ss, bss = ssr.next()
                    k.act(junk[:], xt[:], AF.Square, [bx], [bjunk, bss], accum=ss[:, 0:1])
                    k.act(ss[:, 1:2], ss[:, 0:1], AF.Sqrt, [bss], [bss], scale=1.0 / D, bias=EPS)
                    k.recip(ss[:, 2:3], ss[:, 1:2], [bss], [bss])
                    nbt, bnb = nbr.next()
                    k.ts("vector", nbt[:], xt[:], ss[:, 2:3], None, ALU.mult, None, [bx, bss], [bnb])
                    for kk in range(8):
                        k.tr(ptr[:, kk, :], nbt[:, kk * 128:(kk + 1) * 128], identb[:], [bnb, bid], [bptr])
                    bnTi = nTb[bi][i]
                    k.cp("vector", nTt[:, :, i * 128:(i + 1) * 128], ptr[:], [bptr], [bnTi])
                    for c, (pt, bpt) in enumerate(((pq, bpq), (pk, bpk), (pv, bpv))):
                        k.mm_group([(pt[:], nTt[:, kk, i * 128:(i + 1) * 128], win[:, kk, 512 * (c + 1):512 * (c + 2)],
                                     kk == 0, kk == 7) for kk in range(8)], [bnTi, bwin], [bpt])
                    for which, (pt, bpt, gt, bg, stg, bstg) in enumerate((
                            (pq, bpq, qg, bqg, qts[bi], bqts[bi]), (pk, bpk, kg, bkg, kts[bi], bkts[bi]))):
                        sqf, bsq = sqr.next()
                        k.act(sqf[:], pt[:], AF.Square, [bpt], [bsq])
                        hs, bhs = hsr.next()
                        k.red(hs[:, 0:8], _h8(sqf[:]), [bsq], [bhs])
                        k.act(hs[:, 8:16], hs[:, 0:8], AF.Sqrt, [bhs], [bhs], scale=1.0 / 64, bias=EPS)
                        k.recip(hs[:, 16:24], hs[:, 8:16], [bhs], [bhs])
                        tmp, btm = tmr.next()
                        k.tt("vector", _h8(tmp[:]), _h8(pt[:]), hs[:, 16:24].unsqueeze(2).to_broadcast([128, 8, 64]),
                             ALU.mult, [bpt, bhs], [btm])
                        qn, bqn = qnr.next()
                        k.tt("vector", _h8(qn[:]), _h8(tmp[:]), gt[:].unsqueeze(1).to_broadcast([128, 8, 64]),
                             ALU.mult, [btm, bg], [bqn])
                        for c4 in range(4):
                            k.tr(pqt[:, which, c4, :], qn[:, c4 * 128:(c4 + 1) * 128], identb[:], [bqn, bid],
                                 [bpqt[which]])
                        k.cp("scalar", stg[:, :, i * 128:(i + 1) * 128], pqt[:, which, :, :], [bpqt[which]], [bstg])
                    va, bva = var.next()
                    k.cp("scalar", va[:, :, 0:64], _h8(pv[:]), [bpv], [bva])
                    k.cp("vector", va[:, :, 64:65], vt[:, 0:1].unsqueeze(1).to_broadcast([128, 8, 1]), [bvt], [bva])
                    P.dma("gpsimd", S["VA"][er:er + 128, :], va[:].rearrange("p h e -> p (h e)"), reads=[bva])
                for c in range(4):
                    pu, bpu = pur.next()
                    k.mm_group([(pu[:, 0:W], win[:, kk, c * 128:(c + 1) * 128], nTt[:, kk, 0:W], kk == 0, kk == 7)
                                for kk in range(8)], [bwin] + nTb[bi][0:nt], [bpu])
                    k.cp("scalar" if c % 2 else "vector", uts[bi][:, c, 0:W], pu[:, 0:W], [bpu], [buts[bi]])
                P.dma("gpsimd", UTv[:, :, e0:e0 + W], uts[bi][:, :, 0:W], reads=[buts[bi]])
                P.dma("gpsimd", QTv[:, :, e0:e0 + W], qts[bi][:, :, 0:W], reads=[bqts[bi]])
                P.dma("gpsimd", KTv[:, :, e0:e0 + W], kts[bi][:, :, 0:W], reads=[bkts[bi]])
                e0 += W
        if any(d_["carry"] for d_ in DOMS):
            for srcn, dstn in (("x_pf", "UTPF"), ("x_sf", "UTSF")):
                xsrc = C.inp[srcn]
                for e0 in range(0, PFX, 512):
                    bi = blk_i % 2
                    blk_i += 1
                    nTt = nTs[bi]
                    for i in range(4):
                        er = e0 + i * 128
                        xt, bx = xr.next()
                        P.dma("sync", xt[:], xsrc[er:er + 128, :], writes=[bx])
                        ss, bss = ssr.next()
                        k.act(junk[:], xt[:], AF.Square, [bx], [bjunk, bss], accum=ss[:, 0:1])
                        k.act(ss[:, 1:2], ss[:, 0:1], AF.Sqrt, [bss], [bss], scale=1.0 / D, bias=EPS)
                        k.recip(ss[:, 2:3], ss[:, 1:2], [bss], [bss])
                        nbt, bnb = nbr.next()
                        k.ts("vector", nbt[:], xt[:], ss[:, 2:3], None, ALU.mult, None, [bx, bss], [bnb])
                        for kk in range(8):
                            k.tr(ptr[:, kk, :], nbt[:, kk * 128:(kk + 1) * 128], identb[:], [bnb, bid], [bptr])
                        k.cp("vector", nTt[:, :, i * 128:(i + 1) * 128], ptr[:], [bptr], [nTb[bi][i]])
                    for i in range(4):
                        pu, bpu = pur.next()
                        k.mm_group([(pu[:], nTt[:, kk, i * 128:(i + 1) * 128], win[:, kk, 0:512], kk == 0, kk == 7)
                                    for kk in range(8)], [bwin, nTb[bi][i]], [bpu])
                        k.cp("scalar" if i % 2 else "vector", uts[bi][:, i, :], pu[:], [bpu], [buts[bi]])
                    P.dma("gpsimd", C.scr_pf[dstn][e0:e0 + 512, :].rearrange("(i p) c -> p i c", p=128), uts[bi][:],
                          reads=[buts[bi]])
    P.barrier()


def phase_attn(C):
    nc, P, k = C.nc, C.P, C.k
    with ExitStack() as st:
        T = lambda n, s, d=F32: st.enter_context(nc.sbuf_tensor("at_" + n, s, d))
        PS = lambda n, s, d=F32: st.enter_context(nc.psum_tensor("at_" + n, s, d))
        ar = T("absrel", [128, 2, 128]); bar_ = Buf()
        v01 = T("val01", [128, 2, 128]); bv01 = Buf()
        P.dma("sync", ar[:], C.inp["absrel"][:, :, :], writes=[bar_])
        P.dma("sync", v01[:], C.inp["val01"][:, :, :], writes=[bv01])
        mk = T("mk", [128, 3, 2, 8, 128], BF16); bmk = Buf()
        mtmp = Ring([T(f"mtmp{i}", [128, 2, 128]) for i in range(2)])
        for pi, d in enumerate((1, 4, 16)):
            for h in range(8):
                slope = 2.0 ** (-(h + 1))
                t, b = mtmp.next()
                k.act(t[:], ar[:], AF.Exp, [bar_], [b], scale=-slope * d)
                k.tt("vector", mk[:, pi, :, h, :], t[:], v01[:], ALU.mult, [b, bv01], [bmk])
        MRG = 1024
        WIN = 2048 + 2 * MRG
        KT = T("KTsb", [64, 8, WIN], BF16); bKT = Buf()
        QT = T("QTsb", [64, 8, WIN], BF16); bQT = Buf()
        vtr = Ring([T(f"vt{i}", [128, 8, 65], BF16) for i in range(4)])
        prr = Ring([T(f"pt{i}", [128, 4, 128], BF16) for i in range(8)])
        osr = Ring([T(f"os{i}", [128, 8, 65]) for i in range(3)])
        psr = Ring([PS(f"pss{i}", [128, 4, 128]) for i in range(6)])
        por = Ring([PS(f"pso{i}", [128, 512]) for i in range(2)])
        for dom in DOMS:
            S = C.scr[dom["name"]]
            for sb in range(0, dom["NP"], 2048):
                NPs = min(2048, dom["NP"] - sb)
                w0 = dom["p_lo"] + sb - MRG
                Wn = NPs + 2 * MRG
                p_lo = MRG
                P.dma("sync", KT[:, :, 0:Wn], S["KT"].rearrange("(h p) e -> p h e", p=64)[:, :, w0:w0 + Wn], writes=[bKT])
                P.dma("sync", QT[:, :, 0:Wn], S["QT"].rearrange("(h p) e -> p h e", p=64)[:, :, w0:w0 + Wn], writes=[bQT])
                for pi, d in enumerate((1, 4, 16)):
                    Od = S["O%d" % d]
                    for r in range(d):
                        a0 = p_lo // d
                        npos = NPs // d
                        blocks = [(a, min(128, a0 + npos - a)) for a in range(a0, a0 + npos, 128)]
                        vtiles = {}

                        def load_v(mi, nk):
                            vt, bvt = vtr.next()
                            e_start = w0 + r + d * (a0 - 64 + 128 * mi)
                            P.dma("sync", vt[0:nk].rearrange("p h e -> p (h e)"), S["VA"][sl(e_start, nk, d), :], writes=[bvt])
                            vtiles[mi] = (vt, bvt)

                        load_v(0, 128)
                        for bi_, (a, L) in enumerate(blocks):
                            load_v(bi_ + 1, L)
                            osb, bos = osr.next()
                            units = []
                            for hg in range(2):
                                for m in range(2):
                                    nk = 128 if m == 0 else L
                                    pss, bps = psr.next()
                                    for hh in range(4):
                                        h = hg * 4 + hh
                                        kc = sl(r + d * (a - 64 + 128 * m), nk, d)
                                        qc = sl(r + d * a, L, d)
                                        k.mm(pss[0:nk, hh, 0:L], KT[:, h, kc], QT[:, h, qc], True, True,
                                             [bKT, bQT], [bps])
                                    units.append((hg, m, nk, pss, bps))
                            pts = {}
                            for (hg, m, nk, pss, bps) in units:
                                pt, bp = prr.next()
                                k.act(pt[0:nk, :, 0:L], pss[0:nk, :, 0:L], AF.Exp, [bps], [bp])
                                k.tt("vector", pt[0:nk, :, 0:L], pt[0:nk, :, 0:L],
                                     mk[0:nk, pi, m, hg * 4:(hg + 1) * 4, 0:L], ALU.mult, [bp, bmk], [bp])
                                pts[(hg, m)] = (pt, bp, nk)
                            for hg in range(2):
                                pso_, bpso = por.next()
                                pso = pso_[:, 0:260].rearrange("p (h e) -> p h e", h=4)
                                for m in range(2):
                                    pt, bp, nk = pts[(hg, m)]
                                    vt, bvt = vtiles[bi_ + m]
                                    for hh in range(4):
                                        h = hg * 4 + hh
                                        k.mm(pso[0:L, hh, :], pt[0:nk, hh, 0:L], vt[0:nk, h, :], m == 0 and hh == 0, m == 1,
                                             [bp, bvt], [bpso])
                                k.cp("vector", osb[0:L, hg * 4:(hg + 1) * 4, :], pso[0:L, :, :], [bpso], [bos])
                            P.dma("gpsimd", Od[sl(sb + r + d * a - p_lo, L, d), :], osb[0:L].rearrange("p h e -> p (h e)"),
                                  reads=[bos])
                            del vtiles[bi_]
    P.barrier()


def scratch_spec(dom):
    NE, NP, N = dom["NE"], dom["NP"], dom["N"]
    return {
        "UT": ([512, NE], BF16), "QT": ([512, NE], BF16), "KT": ([512, NE], BF16), "VA": ([NE, 520], BF16),
        "O1": ([NP, 520], F32), "O4": ([NP, 520], F32), "O16": ([NP, 520], F32),
        "YF": ([512, NP], F32), "AN": ([512, NP], BF16), "X1": ([N + 256, D], F32),
    }


PHASES = ["proj", "attn", "s5prep", "prefix", "s5", "comb", "ffn"]


def build_program():
    global DOMS, ALL_DOMS
    if ALL_DOMS is None:
        ALL_DOMS = list(DOMS)
    DOMS[:] = [d_ for d_ in ALL_DOMS if DEBUG["doms"] is None or d_["name"] in DEBUG["doms"]]
    nc = bass.Bass("TRN2", target_bir_lowering=False)
    C = Ctx()
    C.nc = nc
    skip = set()
    for d_ in ALL_DOMS:
        if d_ not in DOMS:
            skip.add("x_" + d_["name"])
            if d_["name"] == "s":
                skip.update(("x_pf", "x_sf"))
    C.inp = {n: nc.dram_tensor(n, shp, F32, kind="ExternalInput").ap() for n, shp in INPUT_SHAPES.items()
             if n not in skip}
    global LAST_INP_NAMES
    LAST_INP_NAMES = list(C.inp.keys())
    C.out = {d_["name"]: nc.dram_tensor("y_" + d_["name"], [d_["N"], D], F32, kind="ExternalOutput").ap() for d_ in DOMS}
    so = DEBUG["scratch_out"] or ()
    C.scr = {}
    for d_ in DOMS:
        C.scr[d_["name"]] = {n: nc.dram_tensor(f"scr_{d_['name']}_{n}", shp, dt,
                                               kind="ExternalOutput" if f"{d_['name']}_{n}" in so else "Internal").ap()
                             for n, (shp, dt) in scratch_spec(d_).items()}
    C.scr_tab = [nc.dram_tensor(f"scr_tab{d}", [128, 5 * 16 * 128], F32, kind="Internal").ap() for d in range(2)]
    C.scr_pf = {nm: nc.dram_tensor("scr_" + nm, [PFX, 512], BF16, kind="Internal").ap() for nm in ("UTPF", "UTSF")}
    C.scr_lam = [nc.dram_tensor(f"scr_lam{d}", [128, 2 * 2 * 16 * 128], BF16, kind="Internal").ap() for d in range(2)]
    phases = DEBUG["phases"] or PHASES
    with ExitStack() as st:
        P = Prog(nc, st)
        C.P = P
        C.k = K(P)
        C.st = st
        idf = st.enter_context(nc.sbuf_tensor("idf", [128, 128], F32))
        C.identb = st.enter_context(nc.sbuf_tensor("identb", [128, 128], BF16))
        C.bid = Buf()
        P.dma("sync", idf[:], C.inp["ident"][:, :], writes=[C.bid])
        C.k.cp("vector", C.identb[:], idf[:], [C.bid], [C.bid])
        C.idf = idf
        fns = {"proj": phase_proj, "attn": phase_attn, "s5prep": phase_s5prep, "prefix": phase_prefix,
               "s5": phase_s5, "comb": phase_comb, "ffn": phase_ffn}
        for ph in PHASES:
            if ph in phases:
                fns[ph](C)
        P.finish()
        P.emit()
    return nc


_NC_CACHE = {}
LAST_INP_NAMES = []


def kernel(**inputs):
    maps = host_layout(inputs)
    key = (tuple(DEBUG["scratch_out"] or ()), tuple(DEBUG["phases"] or PHASES), tuple(DEBUG["doms"] or ()))
    if key not in _NC_CACHE:
        nc_ = build_program()
        _NC_CACHE[key] = (nc_, list(LAST_INP_NAMES))
    nc, names = _NC_CACHE[key]
    maps = [{n: m[n] for n in names} for m in maps]
    res = run_bass_kernel_spmd(nc, maps, core_ids=list(range(NCORES)))
    R = res.results
    kernel.last_results = R
    yp = np.zeros((16, 2048, D), np.float32)
    ys = np.zeros((2, 16384, D), np.float32)
    for c in range(NCORES):
        if "y_p0" in R[c]:
            yp[2 * c] = R[c]["y_p0"]
        if "y_p1" in R[c]:
            yp[2 * c + 1] = R[c]["y_p1"]
        if "y_s" in R[c]:
            ys[c // 4, 4096 * (c % 4):4096 * (c % 4 + 1)] = R[c]["y_s"]
    return (yp, ys)


def phase_s5prep(C):
    pass


def phase_prefix(C):
    pass


TWO_PI = 2.0 * math.pi
PI_SAFE = 3.141592


def phase_s5(C):
    nc, P, k = C.nc, C.P, C.k
    with ExitStack() as st:
        T = lambda n, s, d=F32: st.enter_context(nc.sbuf_tensor("s5_" + n, s, d))
        PS = lambda n, s, d=F32: st.enter_context(nc.psum_tensor("s5_" + n, s, d))
        V = "vector"
        LR = T("LR", [128, 32]); LI = T("LI", [128, 32])
        MUr = T("MUr", [128, 32]); MUi = T("MUi", [128, 32])
        BB = T("BB", [128, 2, 2, 16, 32]); bBB = Buf()
        btab = Buf()
        pst = ExitStack()
        TP = lambda n, s, d=F32: pst.enter_context(nc.sbuf_tensor("s5p_" + n, s, d))
        Ec = TP("Ec", [128, 16, 128]); Es = TP("Es", [128, 16, 128])
        t1 = TP("t1", [128, 16, 128]); t2 = TP("t2", [128, 16, 128])
        rr = TP("rr", [128, 16, 128]); ri = TP("ri", [128, 16, 128])
        tabp = TP("tabp", [128, 5, 16, 128]); btabp = Buf()
        Pre = TP("Pre", [128, 16, GB]); Pim = TP("Pim", [128, 16, GB])
        q1 = TP("q1", [128, 16, GB]); q2 = TP("q2", [128, 16, GB])
        Pbr = TP("Pbr", [128, 16, GB], BF16); Pbi = TP("Pbi", [128, 16, GB], BF16)
        lamt = TP("lamt", [128, 2, 2, 16, 128], BF16); blam = Buf()
        bgr = TP("bgr", [128, 16, 32]); bgi = TP("bgi", [128, 16, 32])
        plam = pst.enter_context(nc.psum_tensor("s5p_plam", [128, 8, 128], BF16)); bplam = Buf()
        for d in range(2):
            sm = {}
            bsm = Buf()
            for nm in ("a_re", "a_im", "log_dt"):
                sm[nm] = TP(f"{nm}{d}", [128, 16])
                P.dma("sync", sm[nm][:], C.inp[nm][d, :, :], writes=[bsm])
            def S_(nm):
                sm[nm] = TP(f"{nm}{d}", [128, 16])
                return sm[nm]
            r_, w_ = [bsm], [bsm]
            k.act(S_("dt")[:], sm["log_dt"][:], AF.Exp, r_, w_)
            k.tt(V, S_("xr")[:], sm["a_re"][:], sm["dt"][:], ALU.mult, r_, w_)
            k.act(S_("mag")[:], sm["xr"][:], AF.Exp, r_, w_)
            k.tt(V, S_("ang")[:], sm["a_im"][:], sm["dt"][:], ALU.mult, r_, w_)
            ki = pst.enter_context(nc.sbuf_tensor(f"s5p_ki{d}", [128, 16], I32))
            for nm, off in (("sin", 0.0), ("cos", math.pi / 2)):
                k.ts(V, S_("t_" + nm)[:], sm["ang"][:], 1.0 / TWO_PI, off / TWO_PI, ALU.mult, ALU.add, r_, w_)
                k.cp(V, ki[:], sm["t_" + nm][:], r_, w_)
                k.cp(V, S_("kf_" + nm)[:], ki[:], r_, w_)
                k.ts(V, S_("a2_" + nm)[:], sm["ang"][:], off, None, ALU.add, None, r_, w_)
                k.stt(S_("r_" + nm)[:], sm["kf_" + nm][:], -TWO_PI, sm["a2_" + nm][:], ALU.mult, ALU.add, r_, w_)
                k.ts(V, sm["r_" + nm][:], sm["r_" + nm][:], -PI_SAFE, PI_SAFE, ALU.max, ALU.min, r_, w_)
                k.act(S_(nm)[:], sm["r_" + nm][:], AF.Sin, r_, w_)
            k.tt(V, LR[:, d * 16:(d + 1) * 16], sm["mag"][:], sm["cos"][:], ALU.mult, r_ + [btab], [btab])
            k.tt(V, LI[:, d * 16:(d + 1) * 16], sm["mag"][:], sm["sin"][:], ALU.mult, r_ + [btab], [btab])
            lr = LR[:, d * 16:(d + 1) * 16]; li = LI[:, d * 16:(d + 1) * 16]
            r2 = r_ + [btab]
            k.tt(V, S_("den")[:], sm["a_re"][:], sm["a_re"][:], ALU.mult, r2, w_)
            k.tt(V, S_("den2")[:], sm["a_im"][:], sm["a_im"][:], ALU.mult, r2, w_)
            k.tt(V, sm["den"][:], sm["den"][:], sm["den2"][:], ALU.add, r2, w_)
            k.recip(S_("rden")[:], sm["den"][:], r2, w_)
            k.ts(V, S_("nr")[:], lr, -1.0, None, ALU.add, None, r2, w_)
            k.tt(V, S_("p1")[:], sm["nr"][:], sm["a_re"][:], ALU.mult, r2, w_)
            k.tt(V, S_("p2")[:], li, sm["a_im"][:], ALU.mult, r2, w_)
            k.tt(V, sm["p1"][:], sm["p1"][:], sm["p2"][:], ALU.add, r2, w_)
            k.tt(V, S_("fr")[:], sm["p1"][:], sm["rden"][:], ALU.mult, r2, w_)
            k.tt(V, S_("p3")[:], li, sm["a_re"][:], ALU.mult, r2, w_)
            k.tt(V, S_("p4")[:], sm["nr"][:], sm["a_im"][:], ALU.mult, r2, w_)
            k.tt(V, sm["p3"][:], sm["p3"][:], sm["p4"][:], ALU.subtract, r2, w_)
            k.tt(V, S_("fi")[:], sm["p3"][:], sm["rden"][:], ALU.mult, r2, w_)
            k.memset(V, Ec[:, :, 0:1], 1.0, w_)
            k.memset(V, Es[:, :, 0:1], 0.0, w_)
            k.cp(V, Ec[:, :, 1:2], sm["cos"][:].unsqueeze(2), r_, w_)
            k.cp(V, Es[:, :, 1:2], sm["sin"][:].unsqueeze(2), r_, w_)
            m = 1
            while m < 127:
                n = min(m, 127 - m)
                cm = Ec[:, :, m:m + 1].to_broadcast([128, 16, n])
                smm = Es[:, :, m:m + 1].to_broadcast([128, 16, n])
                k.tt(V, t1[:, :, 0:n], Ec[:, :, 1:1 + n], cm, ALU.mult, r_, w_)
                k.tt(V, t2[:, :, 0:n], Es[:, :, 1:1 + n], smm, ALU.mult, r_, w_)
                k.tt(V, Ec[:, :, m + 1:m + 1 + n], t1[:, :, 0:n], t2[:, :, 0:n], ALU.subtract, r_, w_)
                k.tt(V, t1[:, :, 0:n], Ec[:, :, 1:1 + n], smm, ALU.mult, r_, w_)
                k.tt(V, t2[:, :, 0:n], Es[:, :, 1:1 + n], cm, ALU.mult, r_, w_)
                k.tt(V, Es[:, :, m + 1:m + 1 + n], t1[:, :, 0:n], t2[:, :, 0:n], ALU.add, r_, w_)
                m += n
            frb = sm["fr"][:].unsqueeze(2).to_broadcast([128, 16, 128])
            fib = sm["fi"][:].unsqueeze(2).to_broadcast([128, 16, 128])
            k.tt(V, t1[:], Ec[:], frb, ALU.mult, r_, w_)
            k.tt(V, t2[:], Es[:], fib, ALU.mult, r_, w_)
            k.tt(V, rr[:], t1[:], t2[:], ALU.add, r_, w_)
            k.tt(V, t1[:], Ec[:], fib, ALU.mult, r_, w_)
            k.tt(V, t2[:], Es[:], frb, ALU.mult, r_, w_)
            k.tt(V, ri[:], t1[:], t2[:], ALU.subtract, r_, w_)
            magb = sm["mag"][:].unsqueeze(2).to_broadcast([128, 16, 127])
            rw = r_ + [btabp]
            k.memset(V, tabp[:, 4, :, :], 0.0, [btabp])
            if d == 0:
                k.cp(V, tabp[:, 0, :, :], rr[:], rw, [btabp])
                k.cp(V, tabp[:, 1, :, :], ri[:], rw, [btabp])
                k.cp(V, tabp[:, 2, :, :], Ec[:], rw, [btabp])
                k.cp(V, tabp[:, 3, :, :], Es[:], rw, [btabp])
                k.cp(V, tabp[:, 4, :, 1:128], magb, rw, [btabp])
            else:
                k.cp(V, tabp[:, 0, :, :], rr[:, :, ::-1], rw, [btabp])
                k.cp(V, tabp[:, 1, :, :], ri[:, :, ::-1], rw, [btabp])
                k.cp(V, tabp[:, 2, :, :], Ec[:, :, ::-1], rw, [btabp])
                k.cp(V, tabp[:, 3, :, :], Es[:, :, ::-1], rw, [btabp])
                k.cp(V, tabp[:, 4, :, 0:127], magb, rw, [btabp])
            P.dma("gpsimd", C.scr_tab[d], tabp[:].rearrange("p a j t -> p (a j t)"), reads=[btabp])
            lrb = lr.unsqueeze(2); lib = li.unsqueeze(2)
            k.memset(V, Pre[:, :, 0:1], 1.0, w_)
            k.memset(V, Pim[:, :, 0:1], 0.0, w_)
            k.cp(V, Pre[:, :, 1:2], lrb, r2, w_)
            k.cp(V, Pim[:, :, 1:2], lib, r2, w_)
            m = 1
            while m < GB - 1:
                n = min(m, GB - 1 - m)
                cm = Pre[:, :, m:m + 1].to_broadcast([128, 16, n])
                smm = Pim[:, :, m:m + 1].to_broadcast([128, 16, n])
                k.tt(V, q1[:, :, 0:n], Pre[:, :, 1:1 + n], cm, ALU.mult, r_, w_)
                k.tt(V, q2[:, :, 0:n], Pim[:, :, 1:1 + n], smm, ALU.mult, r_, w_)
                k.tt(V, Pre[:, :, m + 1:m + 1 + n], q1[:, :, 0:n], q2[:, :, 0:n], ALU.subtract, r_, w_)
                k.tt(V, q1[:, :, 0:n], Pre[:, :, 1:1 + n], smm, ALU.mult, r_, w_)
                k.tt(V, q2[:, :, 0:n], Pim[:, :, 1:1 + n], cm, ALU.mult, r_, w_)
                k.tt(V, Pim[:, :, m + 1:m + 1 + n], q1[:, :, 0:n], q2[:, :, 0:n], ALU.add, r_, w_)
                m += n
            mr = MUr[:, d * 16:(d + 1) * 16]; mi = MUi[:, d * 16:(d + 1) * 16]
            k.tt(V, q1[:, :, 0], Pre[:, :, GB - 1], lr, ALU.mult, r2, w_)
            k.tt(V, q2[:, :, 0], Pim[:, :, GB - 1], li, ALU.mult, r2, w_)
            k.tt(V, mr, q1[:, :, 0], q2[:, :, 0], ALU.subtract, r_, [bsm, btab])
            k.tt(V, q1[:, :, 0], Pre[:, :, GB - 1], li, ALU.mult, r2, w_)
            k.tt(V, q2[:, :, 0], Pim[:, :, GB - 1], lr, ALU.mult, r2, w_)
            k.tt(V, mi, q1[:, :, 0], q2[:, :, 0], ALU.add, r_, [bsm, btab])
            if d == 0:
                k.cp(V, Pbr[:], Pre[:, :, ::-1], r_, w_)
                k.cp(V, Pbi[:], Pim[:, :, ::-1], r_, w_)
            else:
                k.cp(V, Pbr[:], Pre[:], r_, w_)
                k.cp(V, Pbi[:], Pim[:], r_, w_)
            for rix, Pb in enumerate((Pbr, Pbi)):
                for kt in range(GB // 128):
                    for j0 in (0, 8):
                        for jj in range(8):
                            k.tr(plam[:, jj, :], Pb[:, j0 + jj, kt * 128:(kt + 1) * 128], C.identb[:], [bsm, C.bid],
                                 [bplam])
                        k.cp("scalar", lamt[:, rix, kt, j0:j0 + 8, :], plam[:], [bplam], [blam])
            P.dma("gpsimd", C.scr_lam[d], lamt[:].rearrange("p r k j s -> p (r k j s)"), reads=[blam])
            P.dma("sync", bgr[:], C.inp["bg_re"][d], writes=[bsm])
            P.dma("sync", bgi[:], C.inp["bg_im"][d], writes=[bsm])
            frb32 = sm["fr"][:].unsqueeze(2).to_broadcast([128, 16, 32])
            fib32 = sm["fi"][:].unsqueeze(2).to_broadcast([128, 16, 32])
            k.tt(V, q1[:, :, 0:32], bgr[:], frb32, ALU.mult, r_, w_)
            k.tt(V, q2[:, :, 0:32], bgi[:], fib32, ALU.mult, r_, w_)
            k.tt(V, BB[:, d, 0], q1[:, :, 0:32], q2[:, :, 0:32], ALU.subtract, r_, [bBB])
            k.tt(V, q1[:, :, 0:32], bgi[:], frb32, ALU.mult, r_, w_)
            k.tt(V, q2[:, :, 0:32], bgr[:], fib32, ALU.mult, r_, w_)
            k.tt(V, BB[:, d, 1], q1[:, :, 0:32], q2[:, :, 0:32], ALU.add, r_, [bBB])
        P.barrier()
        pst.close()
        Bre = T("Bre", [128, 1, 16, 128], BF16); Bim = T("Bim", [128, 1, 16, 128], BF16)
        Cre = T("Cre", [128, 1, 16, 128], BF16); Cim = T("Cim", [128, 1, 16, 128], BF16)
        bw = Buf()
        wglu = T("wglu", [128, 4, 512], BF16)
        P.dma("gpsimd", wglu[:], C.inp["w_glu"].rearrange("(c p) n -> p c n", p=128), writes=[bw])
        bglu = T("bglu", [128, 4]); Dv = T("Dv", [128, 4])
        P.dma("sync", bglu[:], C.inp["b_glu"][:, :], writes=[bw])
        P.dma("sync", Dv[:], C.inp["s5_d"][:, :], writes=[bw])
        ones = T("ones", [128, 128], BF16)
        k.memset(V, ones[:], 1.0, [bw])
        TB = T("TB", [128, 5, 16, 128])
        Rre, Rim, Ere, Eim, DEC = (TB[:, a_, :, :] for a_ in range(5))
        Eb = T("Eb", [128, 2, 16, 128], BF16); bEb = Buf()
        Gbr = T("Gbr", [128, 16, 128], BF16); Gbi = T("Gbi", [128, 16, 128], BF16); bGb = Buf()
        tC = T("tC", [128, 16, 128], BF16); tD = T("tD", [128, 16, 128], BF16); btp = Buf()
        utr = Ring([T(f"ut{i}", [128, 4, 512], BF16) for i in range(2)])
        BUr = T("BUr", [128, 16, 128], BF16); bBUr = Buf()
        BUi = T("BUi", [128, 16, 128], BF16); bBUi = Buf()
        Rb = T("Rb", [128, 2, 16, 128], BF16); bRb = Buf()
        tB = T("tB", [128, 16, 128], BF16)
        Xre = T("Xre", [128, 16, 128]); bXr = Buf()
        Xim = T("Xim", [128, 16, 128]); bXi = Buf()
        Grr = Ring([T(f"Gre{i}", [128, 16, 128]) for i in range(1)])
        Gir = Ring([T(f"Gim{i}", [128, 16, 128]) for i in range(1)])
        lam = T("lam", [128, 2, 2, 16, 128], BF16); blm = Buf()
        utkr = Ring([T(f"utk{i}", [128, GB // 128, 512], BF16) for i in range(2)])
        w1 = T("w1", [128, 16, 32]); w2 = T("w2", [128, 16, 32]); bw12 = Buf()
        Sre = T("Sre", [128, 16]); Sim = T("Sim", [128, 16]); bS = Buf()
        Hre = T("Hre", [128, 16, 128], BF16); bHr = Buf()
        Him = T("Him", [128, 16, 128], BF16); bHi = Buf()
        tA = T("tA", [128, 16, 128], BF16); btv = Buf()
        cs = T("cs", [128, 16, 4]); bcs = Buf()
        HLre = T("HLre", [128, 16]); HLim = T("HLim", [128, 16]); bHL = Buf()
        ystr = Ring([T(f"yst{i}", [128, 4, 128]) for i in range(1)])
        yflr = Ring([T(f"yfl{i}", [128, 4, 128]) for i in range(2)])
        y2r = Ring([T(f"y2{i}", [128, 4, 128]) for i in range(1)])
        g1 = T("g1", [128, 4, 128]); g2 = T("g2", [128, 4, 128]); bg1 = Buf()
        zr = Ring([T(f"z{i}", [128, 4, 128], BF16) for i in range(1)])
        gate = g2; bgate = bg1
        aa = T("aa", [128, 4, 128]); baa = Buf()
        sqb = T("sqb", [128, 4, 128], BF16); bsqb = Buf()
        rt = T("rt", [128, 128]); brt = Buf()
        anr = Ring([T(f"an{i}", [128, 4, 128], BF16) for i in range(1)])
        psRr = Ring([PS(f"psR{i}", [128, 4, 128]) for i in range(2)])
        psIr = Ring([PS(f"psI{i}", [128, 4, 128]) for i in range(2)])
        psYr = Ring([PS(f"psY{i}", [128, 512]) for i in range(2)])
        pg = PS("pg", [128, 4, 128]); bpg = Buf()
        pss = PS("pss", [128, 512]); bpss = Buf()
        fl = lambda ap: ap.rearrange("p a t -> p (a t)")
        G = "gpsimd"
        for d in range(2):
            dsl = slice(d * 16, (d + 1) * 16)
            P.dma("sync", TB[:].rearrange("p a j t -> p (a j t)"), C.scr_tab[d], writes=[btab])
            P.dma("gpsimd", Bre[:, 0], C.inp["bl_re"][d], writes=[bw])
            P.dma("gpsimd", Bim[:, 0], C.inp["bl_im"][d], writes=[bw])
            P.dma("gpsimd", Cre[:, 0], C.inp["cl_re"][d], writes=[bw])
            P.dma("gpsimd", Cim[:, 0], C.inp["cl_im"][d], writes=[bw])
            k.ts("gpsimd", Cim[:], Cim[:], -1.0, 1.0, ALU.mult, ALU.mult, [bw], [bw])
            k.cp("scalar", Eb[:, 0], Ere, [btab], [bEb])
            k.cp("scalar", Eb[:, 1], Eim, [btab], [bEb])
            k.cp("scalar", Rb[:, 0], Rre, [btab], [bRb])
            k.cp("scalar", Rb[:, 1], Rim, [btab], [bRb])
            for dom in DOMS:
                S = C.scr[dom["name"]]
                NP, p_lo = dom["NP"], dom["p_lo"]
                UTv = S["UT"].rearrange("(c p) e -> p c e", p=128)
                YFv = S["YF"].rearrange("(c p) e -> p c e", p=128)
                ANv = S["AN"].rearrange("(c p) e -> p c e", p=128)
                npc = NP // 128
                k.memset(V, HLre[:], 0.0, [bHL])
                k.memset(V, HLim[:], 0.0, [bHL])
                order = list(range(npc)) if d == 0 else list(range(npc - 1, -1, -1))
                items = []
                if dom["carry"]:
                    P.dma("sync", lam[:].rearrange("p r k j s -> p (r k j s)"), C.scr_lam[d], writes=[blm])
                    UTK = C.scr_pf["UTPF" if d == 0 else "UTSF"]
                    blks = list(range(PFX // GB))
                    if d == 1:
                        blks.reverse()
                    mr = MUr[:, dsl]; mi = MUi[:, dsl]
                    for blk in blks:
                        utk, butk = utkr.next()
                        P.dma("sync", utk[:], UTK[blk * GB:(blk + 1) * GB, :].rearrange("(k p) c -> p k c", p=128),
                              writes=[butk])
                        psR, bpsR = psRr.next()
                        psI, bpsI = psIr.next()
                        gps = (psR[:].rearrange("p a t -> p (a t)"), psI[:].rearrange("p a t -> p (a t)"))
                        gbf = (bpsR, bpsI)
                        for j in range(16):
                            for ri in range(2):
                                for kt in range(GB // 128):
                                    k.mm(gps[ri][:, 32 * j:32 * j + 32], lam[:, ri, kt, j, :], utk[:, kt, 32 * j:32 * j + 32],
                                         kt == 0, kt == GB // 128 - 1, [blm, butk], [gbf[ri]])
                        g3 = [g.rearrange("p (j c) -> p j c", c=32) for g in gps]
                        k.tt(V, w1[:], BB[:, d, 0], g3[0], ALU.mult, [bBB, bpsR], [bw12])
                        k.tt(V, w2[:], BB[:, d, 1], g3[1], ALU.mult, [bBB, bpsI], [bw12])
                        k.tt(V, w1[:], w1[:], w2[:], ALU.subtract, [bw12], [bw12])
                        k.red(Sre[:], w1[:], [bw12], [bS])
                        k.tt(V, w1[:], BB[:, d, 0], g3[1], ALU.mult, [bBB, bpsI], [bw12])
                        k.tt(V, w2[:], BB[:, d, 1], g3[0], ALU.mult, [bBB, bpsR], [bw12])
                        k.tt(V, w1[:], w1[:], w2[:], ALU.add, [bw12], [bw12])
                        k.red(Sim[:], w1[:], [bw12], [bS])
                        k.tt(V, cs[:, :, 0], mr, HLre[:], ALU.mult, [btab, bHL], [bcs])
                        k.tt(V, cs[:, :, 1], mi, HLim[:], ALU.mult, [btab, bHL], [bcs])
                        k.tt(V, cs[:, :, 2], mr, HLim[:], ALU.mult, [btab, bHL], [bcs])
                        k.tt(V, cs[:, :, 3], mi, HLre[:], ALU.mult, [btab, bHL], [bcs])
                        k.tt(V, HLre[:], cs[:, :, 0], cs[:, :, 1], ALU.subtract, [bcs], [bHL])
                        k.tt(V, HLre[:], HLre[:], Sre[:], ALU.add, [bS, bHL], [bHL])
                        k.tt(V, HLim[:], cs[:, :, 2], cs[:, :, 3], ALU.add, [bcs], [bHL])
                        k.tt(V, HLim[:], HLim[:], Sim[:], ALU.add, [bS, bHL], [bHL])
                items += [("main", pi_) for pi_ in order]
                cur_chunk = None
                for kind, pi in items:
                    ci = pi // 4
                    if (kind, ci) != cur_chunk:
                        ut, but = utr.next()
                        P.dma("sync", ut[:], UTv[:, :, p_lo + ci * 512:p_lo + ci * 512 + 512], writes=[but])
                        cur_chunk = (kind, ci)
                    col = (pi % 4) * 128
                    if kind == "pre":
                        pass
                    elif d == 0:
                        yst, byst = ystr.next()
                    else:
                        yfl, byfl = yflr.next()
                        P.dma("sync", yfl[:], YFv[:, :, pi * 128:(pi + 1) * 128], writes=[byfl])
                        y2, by2 = y2r.next()
                    for b in range(4):
                        psR, bpsR = psRr.next()
                        psI, bpsI = psIr.next()
                        for i in range(4):
                            j = 4 * b + i
                            k.mm(psR[:, i, :], Bre[:, 0, j, :], ut[:, b, col:col + 128], True, True, [bw, but], [bpsR])
                            k.mm(psI[:, i, :], Bim[:, 0, j, :], ut[:, b, col:col + 128], True, True, [bw, but], [bpsI])
                        k.cp("scalar", BUr[:, 4 * b:4 * b + 4, :], psR[:], [bpsR], [bBUr])
                        k.cp("scalar", BUi[:, 4 * b:4 * b + 4, :], psI[:], [bpsI], [bBUi])
                    k.tt(V, tA[:], Rb[:, 1], BUi[:], ALU.mult, [bRb, bBUi], [btv])
                    k.tt(V, tB[:], Rb[:, 0], BUr[:], ALU.mult, [bRb, bBUr], [btv])
                    k.tt(V, Xre[:], tB[:], tA[:], ALU.subtract, [btv], [bXr])
                    k.tt(V, tA[:], Rb[:, 0], BUi[:], ALU.mult, [bRb, bBUi], [btv])
                    k.tt(V, tB[:], Rb[:, 1], BUr[:], ALU.mult, [bRb, bBUr], [btv])
                    k.tt(V, Xim[:], tA[:], tB[:], ALU.add, [btv], [bXi])
                    c0 = 0 if d == 0 else 127
                    k.tt(V, cs[:, :, 0], LR[:, dsl], HLre[:], ALU.mult, [btab, bHL], [bcs])
                    k.tt(V, cs[:, :, 1], LI[:, dsl], HLim[:], ALU.mult, [btab, bHL], [bcs])
                    k.tt(V, cs[:, :, 2], LR[:, dsl], HLim[:], ALU.mult, [btab, bHL], [bcs])
                    k.tt(V, cs[:, :, 3], LI[:, dsl], HLre[:], ALU.mult, [btab, bHL], [bcs])
                    k.tt(V, Xre[:, :, c0], Xre[:, :, c0], cs[:, :, 0], ALU.add, [bcs, bXr], [bXr])
                    k.tt(V, Xre[:, :, c0], Xre[:, :, c0], cs[:, :, 1], ALU.subtract, [bcs, bXr], [bXr])
                    k.tt(V, Xim[:, :, c0], Xim[:, :, c0], cs[:, :, 2], ALU.add, [bcs, bXi], [bXi])
                    k.tt(V, Xim[:, :, c0], Xim[:, :, c0], cs[:, :, 3], ALU.add, [bcs, bXi], [bXi])
                    Gre, bGr = Grr.next()
                    Gim, bGi = Gir.next()
                    if d == 0:
                        k.scan(fl(Gre[:]), fl(DEC), fl(Xre[:]), [btab, bXr], [bGr])
                        k.scan(fl(Gim[:]), fl(DEC), fl(Xim[:]), [btab, bXi], [bGi])
                    else:
                        k.scan(fl(Gre[:])[:, ::-1], fl(DEC)[:, ::-1], fl(Xre[:])[:, ::-1], [btab, bXr], [bGr])
                        k.scan(fl(Gim[:])[:, ::-1], fl(DEC)[:, ::-1], fl(Xim[:])[:, ::-1], [btab, bXi], [bGi])
                    c1 = 127 if d == 0 else 0
                    k.tt(V, cs[:, :, 0], Ere[:, :, c1], Gre[:, :, c1], ALU.mult, [btab, bGr], [bcs])
                    k.tt(V, cs[:, :, 1], Eim[:, :, c1], Gim[:, :, c1], ALU.mult, [btab, bGi], [bcs])
                    k.tt(V, cs[:, :, 2], Eim[:, :, c1], Gre[:, :, c1], ALU.mult, [btab, bGr], [bcs])
                    k.tt(V, cs[:, :, 3], Ere[:, :, c1], Gim[:, :, c1], ALU.mult, [btab, bGi], [bcs])
                    k.tt(V, HLre[:], cs[:, :, 0], cs[:, :, 1], ALU.subtract, [bcs], [bHL])
                    k.tt(V, HLim[:], cs[:, :, 2], cs[:, :, 3], ALU.add, [bcs], [bHL])
                    if kind == "pre":
                        continue
                    k.cp("scalar", Gbr[:], Gre[:], [bGr], [bGb])
                    k.cp("scalar", Gbi[:], Gim[:], [bGi], [bGb])
                    k.tt(V, tC[:], Eb[:, 0], Gbr[:], ALU.mult, [bEb, bGb], [btp])
                    k.tt(V, tD[:], Eb[:, 1], Gbi[:], ALU.mult, [bEb, bGb], [btp])
                    k.tt(V, Hre[:], tC[:], tD[:], ALU.subtract, [btp], [bHr])
                    k.tt(V, tC[:], Eb[:, 1], Gbr[:], ALU.mult, [bEb, bGb], [btp])
                    k.tt(V, tD[:], Eb[:, 0], Gbi[:], ALU.mult, [bEb, bGb], [btp])
                    k.tt(V, Him[:], tC[:], tD[:], ALU.add, [btp], [bHi])
                    for b in range(4):
                        psY, bpsY = psYr.next()
                        for i in range(4):
                            j = 4 * b + i
                            k.mm(psY[:, 0:128], Cre[:, 0, j, :], Hre[:, j, :], i == 0, False, [bw, bHr], [bpsY])
                            k.mm(psY[:, 0:128], Cim[:, 0, j, :], Him[:, j, :], False, i == 3, [bw, bHi], [bpsY])
                        if d == 0:
                            k.cp("scalar", yst[:, b, :], psY[:, 0:128], [bpsY], [byst])
                        else:
                            k.tt(V, y2[:, b, :], psY[:, 0:128], yfl[:, b, :], ALU.add, [bpsY, byfl], [by2])
                            k.stt(y2[:, b, :], ut[:, b, col:col + 128], Dv[:, b:b + 1], y2[:, b, :], ALU.mult, ALU.add,
                                  [but, bw, by2], [by2])
                    if d == 0:
                        P.dma("gpsimd", YFv[:, :, pi * 128:(pi + 1) * 128], yst[:], reads=[byst])
                        continue
                    G = "gpsimd"
                    k.tt(G, g1[:], y2[:], y2[:], ALU.mult, [by2], [bg1])
                    k.ts(G, g1[:], g1[:], 0.044715, 1.0, ALU.mult, ALU.add, [bg1], [bg1])
                    k.tt(G, g1[:], g1[:], y2[:], ALU.mult, [bg1, by2], [bg1])
                    k.act(g2[:], g1[:], AF.Sigmoid, [bg1], [bg1], scale=2.0 * math.sqrt(2.0 / math.pi))
                    z, bz = zr.next()
                    k.tt(G, z[:], y2[:], g2[:], ALU.mult, [by2, bg1], [bz])
                    for co in range(4):
                        for ci_ in range(4):
                            k.mm(pg[:, co, :], wglu[:, ci_, co * 128:(co + 1) * 128], z[:, ci_, :], ci_ == 0, ci_ == 3,
                                 [bw, bz], [bpg])
                    for co in range(4):
                        k.act(gate[:, co, :], pg[:, co, :], AF.Sigmoid, [bpg, bw], [bgate], bias=bglu[:, co:co + 1])
                    k.tt(G, aa[:], z[:], gate[:], ALU.mult, [bz, bgate], [baa])
                    k.tt(G, sqb[:], aa[:], aa[:], ALU.mult, [baa], [bsqb])
                    for c in range(4):
                        k.mm(pss[:, 0:128], ones[:], sqb[:, c, :], c == 0, c == 3, [bw, bsqb], [bpss])
                    k.act(rt[:], pss[:, 0:128], AF.Sqrt, [bpss], [brt], scale=1.0 / 512, bias=EPS)
                    k.recip(rt[:], rt[:], [brt], [brt])
                    an, ban = anr.next()
                    k.tt(V, an[:], aa[:], rt[:].unsqueeze(1).to_broadcast([128, 4, 128]), ALU.mult, [baa, brt], [ban])
                    P.dma("gpsimd", ANv[:, :, pi * 128:(pi + 1) * 128], an[:], reads=[ban])
            P.barrier()


def phase_comb(C):
    nc, P, k = C.nc, C.P, C.k
    with ExitStack() as st:
        T = lambda n, s, d=F32: st.enter_context(nc.sbuf_tensor("cb_" + n, s, d))
        PS = lambda n, s, d=F32: st.enter_context(nc.psum_tensor("cb_" + n, s, d))
        V, G = "vector", "gpsimd"
        wout = T("wout", [128, 8, 1024], BF16); bwo = Buf()
        go = T("go", [128, 8]); bgo = Buf()
        P.dma("sync", go[:], C.inp["g_out"][:, :], writes=[bgo])
        wst = Ring([T(f"wst{i}", [128, 1024]) for i in range(2)])
        for kk in range(8):
            t, b = wst.next()
            P.dma("sync", t[:], C.inp["w_out"][kk * 128:(kk + 1) * 128, :], writes=[b])
            k.ts(V, wout[:, kk, :], t[:], go[:, kk:kk + 1], None, ALU.mult, None, [b, bgo], [bwo])
        zt = T("zt", [128, 1024]); bzt = Buf()
        k.memset(V, zt[:], 0.0, [bzt])
        o1r = Ring([T(f"o1_{i}", [128, 8, 65]) for i in range(2)])
        o4r = Ring([T(f"o4_{i}", [128, 8, 65]) for i in range(2)])
        o16r = Ring([T(f"o16_{i}", [128, 8, 65]) for i in range(2)])
        xr = Ring([T(f"x{i}", [128, 1024]) for i in range(3)])
        vr = Ring([T(f"vl{i}", [128, 1]) for i in range(3)])
        anr = Ring([T(f"an{i}", [128, 4, 512], BF16) for i in range(2)])
        dnr = Ring([T(f"dn{i}", [128, 16]) for i in range(2)])
        bfr = Ring([T(f"bf{i}", [128, 8, 64]) for i in range(2)])
        junk = T("junk", [128, 512], BF16); bjunk = Buf()
        ssr = Ring([T(f"ss{i}", [128, 4]) for i in range(3)])
        bnr = Ring([T(f"bn{i}", [128, 512], BF16) for i in range(2)])
        bntr = Ring([T(f"bnT{i}", [128, 4, 128], BF16) for i in range(2)])
        x1r = Ring([T(f"x1_{i}", [128, 1024]) for i in range(2)])
        ptb_ = PS("ptb", [128, 8, 128], BF16); bptb = Buf()
        ptb = ptb_[:, 0:4, :]
        pxr = Ring([PS(f"px{i}", [128, 512]) for i in range(4)])
        identb, bid = C.identb, C.bid
        for dom in DOMS:
            S = C.scr[dom["name"]]
            NP, p_lo, N, ext = dom["NP"], dom["p_lo"], dom["N"], dom["ext"]
            xsrc = C.inp["x_" + dom["name"]]
            vsrc = C.inp["valid_s" if dom["name"] == "s" else "valid_p"]
            ANv = S["AN"].rearrange("(c p) e -> p c e", p=128)
            if ext == 0:
                P.dma("gpsimd", S["X1"][0:128, :], zt[:], reads=[bzt])
                P.dma("gpsimd", S["X1"][128 + N:256 + N, :], zt[:], reads=[bzt])
            for ti in range(NP // 128):
                if ti % 4 == 0:
                    anc, banc = anr.next()
                    wcn = min(512, NP - ti * 128)
                    P.dma("sync", anc[:, :, 0:wcn], ANv[:, :, ti * 128:ti * 128 + wcn], writes=[banc])
                col = (ti % 4) * 128
                o1, bo1 = o1r.next(); o4, bo4 = o4r.next(); o16, bo16 = o16r.next()
                rs = slice(ti * 128, ti * 128 + 128)
                P.dma("sync", o1[:].rearrange("p h e -> p (h e)"), S["O1"][rs, :], writes=[bo1])
                P.dma("sync", o4[:].rearrange("p h e -> p (h e)"), S["O4"][rs, :], writes=[bo4])
                P.dma("sync", o16[:].rearrange("p h e -> p (h e)"), S["O16"][rs, :], writes=[bo16])
                xt, bx = xr.next()
                er = p_lo + ti * 128
                P.dma("sync", xt[:], xsrc[er - dom["comp_lo"]:er - dom["comp_lo"] + 128, :], writes=[bx])
                vt, bvt = vr.next()
                P.dma("sync", vt[:], vsrc[er:er + 128, :], writes=[bvt])
                k.tt(V, o1[:], o1[:], o4[:], ALU.add, [bo1, bo4], [bo1])
                k.tt(V, o1[:], o1[:], o16[:], ALU.add, [bo1, bo16], [bo1])
                dn, bdn = dnr.next()
                k.ts(V, dn[:, 0:8], o1[:, :, 64], 1e-30, None, ALU.max, None, [bo1], [bdn])
                k.recip(dn[:, 8:16], dn[:, 0:8], [bdn], [bdn])
                bfp, bbf = bfr.next()
                k.tt(V, bfp[:], o1[:, :, 0:64], dn[:, 8:16].unsqueeze(2).to_broadcast([128, 8, 64]), ALU.mult,
                     [bo1, bdn], [bbf])
                ss, bss = ssr.next()
                k.act(junk[:], bfp[:].rearrange("p h e -> p (h e)"), AF.Square, [bbf], [bjunk, bss], accum=ss[:, 0:1])
                k.act(ss[:, 1:2], ss[:, 0:1], AF.Sqrt, [bss], [bss], scale=1.0 / 512, bias=EPS)
                k.recip(ss[:, 2:3], ss[:, 1:2], [bss], [bss])
                bn, bbn = bnr.next()
                k.act(bn[:], bfp[:].rearrange("p h e -> p (h e)"), AF.Copy, [bbf, bss], [bbn], scale=ss[:, 2:3])
                for c4 in range(4):
                    k.tr(ptb[:, c4, :], bn[:, c4 * 128:(c4 + 1) * 128], identb[:], [bbn, bid], [bptb])
                bnT, bbnT = bntr.next()
                k.cp(V, bnT[:], ptb, [bptb], [bbnT])
                x1t, bx1 = x1r.next()
                for half in range(2):
                    px, bpx = pxr.next()
                    k.mm_group([(px[:], anc[:, kk, col:col + 128] if kk < 4 else bnT[:, kk - 4, :],
                                 wout[:, kk, half * 512:(half + 1) * 512], kk == 0, kk == 7) for kk in range(8)],
                               [banc, bbnT, bwo], [bpx])
                    k.stt(x1t[:, half * 512:(half + 1) * 512], px[:], vt[:, 0:1], xt[:, half * 512:(half + 1) * 512],
                          ALU.mult, ALU.add, [bpx, bvt, bx], [bx1])
                P.dma("gpsimd", S["X1"][128 - ext + ti * 128:128 - ext + ti * 128 + 128, :], x1t[:], reads=[bx1])
    P.barrier()


def phase_ffn(C):
    nc, P, k = C.nc, C.P, C.k
    with ExitStack() as st:
        T = lambda n, s, d=F32: st.enter_context(nc.sbuf_tensor("ff_" + n, s, d))
        PS = lambda n, s, d=F32: st.enter_context(nc.psum_tensor("ff_" + n, s, d))
        V, G = "vector", "gpsimd"
        wup = T("wup", [128, 8, 2 * DFF], BF16); bwu = Buf()
        wdn = T("wdn", [128, NFC, 1024], BF16); bwd = Buf()
        gf = T("gf", [128, 8]); bgf = Buf()
        P.dma("sync", gf[:], C.inp["g_ffn"][:, :], writes=[bgf])
        pst = ExitStack()
        wst = Ring([pst.enter_context(nc.sbuf_tensor(f"ff_wst{i}", [128, 1408], F32)) for i in range(2)])
        for kk in range(8):
            for q4 in range(4):
                t, b = wst.next()
                P.dma("sync", t[:], C.inp["w_up"][kk * 128:(kk + 1) * 128, q4 * 1408:(q4 + 1) * 1408], writes=[b])
                k.ts(V, wup[:, kk, q4 * 1408:(q4 + 1) * 1408], t[:], gf[:, kk:kk + 1], None, ALU.mult, None,
                     [b, bgf], [bwu])
        P.barrier()
        pst.close()
        wdv = C.inp["w_down"].rearrange("(c p) n -> p c n", p=128)
        for c0 in range(0, NFC, 2):
            P.dma("gpsimd", wdn[:, c0:c0 + 2, :], wdv[:, c0:c0 + 2, :], writes=[bwd])
        cw = T("cw", [128, 44, 3]); cb = T("cbias", [128, 44]); bcw = Buf()
        P.dma("sync", cw[:], C.inp["conv_w"][:, :, :], writes=[bcw])
        P.dma("sync", cb[:], C.inp["conv_b"][:, :], writes=[bcw])
        xr = Ring([T(f"x{i}", [128, 1024]) for i in range(2)])
        junk = T("junk", [128, 1024], BF16); bjunk = Buf()
        ssr = Ring([T(f"ss{i}", [128, 4]) for i in range(3)])
        nbr = Ring([T(f"nb{i}", [128, 1024], BF16) for i in range(2)])
        n2T = T("n2T", [128, 8, 512], BF16); bn2T = Buf()
        actT = T("actT", [128, NFC, 512], BF16); bact = Buf()
        cgr = Ring([T(f"cg{i}", [128, 512]) for i in range(2)])
        cur = Ring([T(f"cu{i}", [128, 512]) for i in range(2)])
        sgr = Ring([T(f"sg{i}", [128, 512]) for i in range(2)])
        x2r = Ring([T(f"x2_{i}", [128, 1024]) for i in range(2)])
        ytr = Ring([T(f"yt{i}", [128, 1024]) for i in range(2)])
        ptr = PS("ptr", [128, 8, 128], BF16); bptr = Buf()
        phr = Ring([PS(f"ph{i}", [128, 512]) for i in range(4)])
        pyr = Ring([PS(f"py{i}", [128, 512]) for i in range(2)])
        identb, bid = C.identb, C.bid
        for dom in DOMS:
            S = C.scr[dom["name"]]
            N = dom["N"]
            yout = C.out[dom["name"]]
            X1 = S["X1"]
            for b0 in range(0, N, 510):
                BT = min(510, N - b0)
                NB = BT + 2
                for i in range((NB + 127) // 128):
                    rows = min(128, NB - 128 * i)
                    r0 = 127 + b0 + 128 * i
                    xt, bx = xr.next()
                    P.dma("sync", xt[0:rows, :], X1[r0:r0 + rows, :], writes=[bx])
                    ss, bss = ssr.next()
                    k.act(junk[0:rows, :], xt[0:rows, :], AF.Square, [bx], [bjunk, bss], accum=ss[0:rows, 0:1])
                    k.act(ss[0:rows, 1:2], ss[0:rows, 0:1], AF.Sqrt, [bss], [bss], scale=1.0 / D, bias=EPS)
                    k.recip(ss[0:rows, 2:3], ss[0:rows, 1:2], [bss], [bss])
                    nbt, bnb = nbr.next()
                    k.act(nbt[0:rows, :], xt[0:rows, :], AF.Copy, [bx, bss], [bnb], scale=ss[0:rows, 2:3])
                    for kk in range(8):
                        k.tr(ptr[:, kk, 0:rows], nbt[0:rows, kk * 128:(kk + 1) * 128], identb[0:rows, 0:rows],
                             [bnb, bid], [bptr])
                    k.cp(V, n2T[:, :, 128 * i:128 * i + rows], ptr[:, :, 0:rows], [bptr], [bn2T])
                for j in range(NFC):
                    outs = []
                    for hf in range(2):
                        ch = j + NFC * hf
                        ph, bph = phr.next()
                        k.mm_group([(ph[:, 0:NB], wup[:, kk, ch * 128:(ch + 1) * 128], n2T[:, kk, 0:NB], kk == 0, kk == 7)
                                    for kk in range(8)], [bwu, bn2T], [bph])
                        cg, bcg = (cgr if hf == 0 else cur).next()
                        k.act(cg[:, 0:BT], ph[:, 0:BT], AF.Identity, [bph, bcw], [bcg], scale=cw[:, ch, 0:1],
                              bias=cb[:, ch:ch + 1])
                        k.stt(cg[:, 0:BT], ph[:, 1:BT + 1], cw[:, ch, 1:2], cg[:, 0:BT], ALU.mult, ALU.add,
                              [bph, bcw, bcg], [bcg])
                        k.stt(cg[:, 0:BT], ph[:, 2:BT + 2], cw[:, ch, 2:3], cg[:, 0:BT], ALU.mult, ALU.add,
                              [bph, bcw, bcg], [bcg])
                        outs.append((cg, bcg))
                    sg, bsg = sgr.next()
                    k.act(sg[:, 0:BT], outs[0][0][:, 0:BT], AF.Silu, [outs[0][1]], [bsg])
                    k.tt(G, actT[:, j, 0:BT], sg[:, 0:BT], outs[1][0][:, 0:BT], ALU.mult, [bsg, outs[1][1]], [bact])
                for io in range((BT + 127) // 128):
                    rows = min(128, BT - 128 * io)
                    x2, bx2 = x2r.next()
                    P.dma("sync", x2[0:rows, :], X1[128 + b0 + 128 * io:128 + b0 + 128 * io + rows, :], writes=[bx2])
                    yt, byt = ytr.next()
                    for half in range(2):
                        py, bpy = pyr.next()
                        k.mm_group([(py[0:rows, :], actT[:, j, 128 * io:128 * io + rows],
                                     wdn[:, j, half * 512:(half + 1) * 512], j == 0, j == NFC - 1) for j in range(NFC)],
                                   [bact, bwd], [bpy])
                        k.tt(V, yt[0:rows, half * 512:(half + 1) * 512], py[0:rows, :],
                             x2[0:rows, half * 512:(half + 1) * 512], ALU.add, [bpy, bx2], [byt])
                    P.dma("gpsimd", yout[b0 + 128 * io:b0 + 128 * io + rows, :], yt[0:rows, :], reads=[byt],
                          is_output=True)
    P.barrier()
```
